# Optimizing a Trainium2 kernel written in Bass

```python
import math
import jax, jax.numpy as jnp
from jax import lax
import numpy as np

D_MODEL = 1024
BATCH = 8
SEQ = 2048
DEPTH = 1

MLSTM_HEADS = 4
MLSTM_HEAD_DIM = 256
MLSTM_WIDTH = MLSTM_HEADS * MLSTM_HEAD_DIM
MLSTM_CONV = 4
MLSTM_CHUNK = 64
DIFF_HEADS = 8
DIFF_HEAD_DIM = 64
DIFF_V_DIM = 2 * DIFF_HEAD_DIM
DIFF_QK_WIDTH = 2 * DIFF_HEADS * DIFF_HEAD_DIM
DIFF_WIDTH = DIFF_HEADS * DIFF_V_DIM
Q_BLOCK = 128
REL_BUCKETS = 32
REL_MAX_DIST = 128
D_FF = 2816
FFN_CONV = 3
N_BRANCHES = 2
IN_SPLITS = (MLSTM_WIDTH, MLSTM_WIDTH, MLSTM_WIDTH, MLSTM_WIDTH, MLSTM_HEADS, MLSTM_HEADS, DIFF_QK_WIDTH, DIFF_QK_WIDTH, DIFF_WIDTH, N_BRANCHES * D_MODEL)
N_IN = 4 * MLSTM_WIDTH + 2 * MLSTM_HEADS + 2 * DIFF_QK_WIDTH + DIFF_WIDTH + N_BRANCHES * D_MODEL
DEEPNORM_ALPHA = (2.0 * DEPTH) ** 0.25
DEEPNORM_BETA = (8.0 * DEPTH) ** -0.25
LN_EPS = 1e-5

kernel_name = 'hybrid_mlstm_diffattn_convffn_deepnorm'

F32 = jnp.float32


def layer_norm(t, g, b):
    tf = t.astype(F32)
    mu = tf.mean(-1, keepdims=True)
    var = jnp.square(tf - mu).mean(-1, keepdims=True)
    return ((tf - mu) * lax.rsqrt(var + LN_EPS) * g.astype(F32) + b.astype(F32)).astype(t.dtype)


def causal_depthwise_conv(u, w, b):
    K, C = w.shape
    y = lax.conv_general_dilated(u, w[:, None, :].astype(u.dtype), window_strides=(1,), padding=[(K - 1, 0)], dimension_numbers=('NWC', 'WIO', 'NWC'), feature_group_count=C)
    return y + b.astype(u.dtype)


def t5_bucket(dist):
    n = jnp.maximum(dist, 0)
    max_exact = REL_BUCKETS // 2
    nf = jnp.maximum(n, 1).astype(F32)
    large = max_exact + (jnp.log(nf / max_exact) / math.log(REL_MAX_DIST / max_exact) * (REL_BUCKETS - max_exact)).astype(jnp.int32)
    large = jnp.minimum(large, REL_BUCKETS - 1)
    return jnp.where(n < max_exact, n, large)


def mlstm_chunkwise(q, k, v, i_pre, f_pre):
    B, H, S, d = q.shape
    L = MLSTM_CHUNK
    nc = S // L
    k = k * (d ** -0.5)
    log_f = jax.nn.log_sigmoid(f_pre)
    to_c = lambda t: t.reshape(B, H, nc, L, d).transpose(2, 0, 1, 3, 4)
    to_cg = lambda t: t.reshape(B, H, nc, L).transpose(2, 0, 1, 3)
    tril = jnp.tril(jnp.ones((L, L), dtype=bool))

    def step(carry, inp):
        C, n, m = carry
        qc, kc, vc, ic, lfc = inp
        b = jnp.cumsum(lfc, axis=-1)
        D = jnp.where(tril, b[..., :, None] - b[..., None, :] + ic[..., None, :], -jnp.inf)
        a = b + m[..., None]
        m_row = jnp.maximum(a, D.max(-1))
        W = jnp.exp(D - m_row[..., None])
        inter = jnp.exp(a - m_row)
        sqk = jnp.einsum('bhjd,bhsd->bhjs', qc, kc) * W
        num = inter[..., None] * jnp.einsum('bhjd,bhde->bhje', qc, C) + jnp.einsum('bhjs,bhse->bhje', sqk, vc)
        den = inter * jnp.einsum('bhjd,bhd->bhj', qc, n) + sqk.sum(-1)
        h = num / jnp.maximum(jnp.abs(den), jnp.exp(-m_row))[..., None]
        bL = b[..., -1]
        g = bL[..., None] - b + ic
        m_new = jnp.maximum(bL + m, g.max(-1))
        decay = jnp.exp(bL + m - m_new)
        ws = jnp.exp(g - m_new[..., None])
        C_new = decay[..., None, None] * C + jnp.einsum('bhsd,bhse->bhde', kc * ws[..., None], vc)
        n_new = decay[..., None] * n + jnp.einsum('bhs,bhsd->bhd', ws, kc)
        return (C_new, n_new, m_new), h

    init = (jnp.zeros((B, H, d, d), F32), jnp.zeros((B, H, d), F32), jnp.zeros((B, H), F32))
    _, hs = lax.scan(step, init, (to_c(q), to_c(k), to_c(v), to_cg(i_pre), to_cg(log_f)))
    return hs.transpose(1, 2, 0, 3, 4).reshape(B, H, S, d)


def diff_attention(q1, q2, k1, k2, v, lam, rel_bias):
    B, H, S, dh = q1.shape
    dv = v.shape[-1]
    nb = S // Q_BLOCK
    scale = dh ** -0.5
    k_pos = jnp.arange(S)
    table = rel_bias.astype(F32)

    def block(args):
        qb1, qb2, blk = args
        q_pos = blk * Q_BLOCK + jnp.arange(Q_BLOCK)
        dist = q_pos[:, None] - k_pos[None, :]
        bias = table[t5_bucket(dist)].transpose(2, 0, 1)
        causal = dist >= 0

        def probs(qb, kk):
            s = jnp.einsum('bhqd,bhkd->bhqk', qb, kk) * scale + bias
            return jax.nn.softmax(jnp.where(causal, s, -jnp.inf), axis=-1)

        attn = probs(qb1, k1) - lam * probs(qb2, k2)
        return jnp.einsum('bhqk,bhkv->bhqv', attn, v)

    to_b = lambda t: t.reshape(B, H, nb, Q_BLOCK, dh).transpose(2, 0, 1, 3, 4)
    out = lax.map(block, (to_b(q1), to_b(q2), jnp.arange(nb)))
    return out.transpose(1, 0, 3, 2, 4).reshape(B, S, H, dv)


def token_mixer(x, w_in, b_in, conv_w, conv_b, m_norm_w, lq1, lk1, lq2, lk2, d_norm_w, rel_bias, w_bm, w_bd, w_out, lam_init):
    B, S, _ = x.shape
    proj = x @ w_in + b_in
    splits = np.cumsum(IN_SPLITS)[:-1].tolist()
    mq, mk, mv, mo, mi, mf, dq, dk, dv, gates = jnp.split(proj, splits, axis=-1)

    qk = jax.nn.silu(causal_depthwise_conv(jnp.concatenate([mq, mk], axis=-1), conv_w, conv_b))
    mq, mk = jnp.split(qk, 2, axis=-1)
    mh = lambda t: t.reshape(B, S, MLSTM_HEADS, MLSTM_HEAD_DIM).transpose(0, 2, 1, 3).astype(F32)
    hm = mlstm_chunkwise(mh(mq), mh(mk), mh(mv), mi.transpose(0, 2, 1).astype(F32), mf.transpose(0, 2, 1).astype(F32))
    hm = hm.transpose(0, 2, 1, 3)
    hm = jax.nn.sigmoid(mo.astype(F32)).reshape(B, S, MLSTM_HEADS, MLSTM_HEAD_DIM) * hm
    mu = hm.mean(-1, keepdims=True)
    var = jnp.square(hm - mu).mean(-1, keepdims=True)
    hm = (hm - mu) * lax.rsqrt(var + LN_EPS) * m_norm_w.astype(F32).reshape(MLSTM_HEADS, MLSTM_HEAD_DIM)
    hm = hm.reshape(B, S, MLSTM_WIDTH).astype(x.dtype)

    qq = dq.reshape(B, S, DIFF_HEADS, 2, DIFF_HEAD_DIM).transpose(3, 0, 2, 1, 4).astype(F32)
    kk = dk.reshape(B, S, DIFF_HEADS, 2, DIFF_HEAD_DIM).transpose(3, 0, 2, 1, 4).astype(F32)
    vv = dv.reshape(B, S, DIFF_HEADS, DIFF_V_DIM).transpose(0, 2, 1, 3).astype(F32)
    lam = jnp.exp(jnp.sum(lq1.astype(F32) * lk1.astype(F32))) - jnp.exp(jnp.sum(lq2.astype(F32) * lk2.astype(F32))) + lam_init
    hd = diff_attention(qq[0], qq[1], kk[0], kk[1], vv, lam, rel_bias)
    hd = hd * lax.rsqrt(jnp.square(hd).mean(-1, keepdims=True) + LN_EPS) * d_norm_w.astype(F32)
    hd = (hd * (1.0 - lam_init)).reshape(B, S, DIFF_WIDTH).astype(x.dtype)

    g_m, g_d = jnp.split(jax.nn.sigmoid(gates), N_BRANCHES, axis=-1)
    merged = g_m * (hm @ w_bm) + g_d * (hd @ w_bd)
    return merged @ w_out


def conv_ffn(x, w_up, conv_w, conv_b, w_down):
    a, b = jnp.split(x @ w_up, 2, axis=-1)
    a = causal_depthwise_conv(a, conv_w, conv_b)
    return (jax.nn.silu(a) * b) @ w_down


def setup_inputs(seed: int = 0) -> dict:
    key = jax.random.key(seed)
    ks = jax.random.split(key, 24)

    def nrm(k, shape, scale):
        return jax.random.normal(k, shape, F32) * scale

    beta = DEEPNORM_BETA
    x = nrm(ks[0], (BATCH, SEQ, D_MODEL), 1.0)
    col_scale = jnp.concatenate([
        jnp.ones((2 * MLSTM_WIDTH,), F32),
        jnp.full((MLSTM_WIDTH,), beta, F32),
        jnp.ones((MLSTM_WIDTH + 2 * MLSTM_HEADS + 2 * DIFF_QK_WIDTH,), F32),
        jnp.full((DIFF_WIDTH,), beta, F32),
        jnp.ones((N_BRANCHES * D_MODEL,), F32)])
    w_in = nrm(ks[1], (DEPTH, D_MODEL, N_IN), D_MODEL ** -0.5) * col_scale
    b_in = nrm(ks[2], (DEPTH, N_IN), 0.02)
    f_off = 4 * MLSTM_WIDTH + MLSTM_HEADS
    b_in = b_in.at[:, f_off:f_off + MLSTM_HEADS].add(jnp.linspace(3.0, 6.0, MLSTM_HEADS, dtype=F32))
    mlstm_conv_w = nrm(ks[3], (DEPTH, MLSTM_CONV, 2 * MLSTM_WIDTH), MLSTM_CONV ** -0.5)
    mlstm_conv_b = nrm(ks[4], (DEPTH, 2 * MLSTM_WIDTH), 0.02)
    mlstm_norm_w = 1.0 + nrm(ks[5], (DEPTH, MLSTM_WIDTH), 0.02)
    lambda_q1 = nrm(ks[6], (DEPTH, DIFF_HEAD_DIM), 0.1)
    lambda_k1 = nrm(ks[7], (DEPTH, DIFF_HEAD_DIM), 0.1)
    lambda_q2 = nrm(ks[8], (DEPTH, DIFF_HEAD_DIM), 0.1)
    lambda_k2 = nrm(ks[9], (DEPTH, DIFF_HEAD_DIM), 0.1)
    diff_norm_w = 1.0 + nrm(ks[10], (DEPTH, DIFF_V_DIM), 0.02)
    rel_bias = nrm(ks[11], (REL_BUCKETS, DIFF_HEADS), 0.2)
    w_branch_mlstm = nrm(ks[12], (DEPTH, MLSTM_WIDTH, D_MODEL), MLSTM_WIDTH ** -0.5 * beta)
    w_branch_diff = nrm(ks[13], (DEPTH, DIFF_WIDTH, D_MODEL), DIFF_WIDTH ** -0.5 * beta)
    w_out = nrm(ks[14], (DEPTH, D_MODEL, D_MODEL), D_MODEL ** -0.5 * beta)
    ln1_g = 1.0 + nrm(ks[15], (DEPTH, D_MODEL), 0.02)
    ln1_b = nrm(ks[16], (DEPTH, D_MODEL), 0.02)
    w_ffn_up = nrm(ks[17], (DEPTH, D_MODEL, 2 * D_FF), D_MODEL ** -0.5 * beta)
    ffn_conv_w = nrm(ks[18], (DEPTH, FFN_CONV, D_FF), FFN_CONV ** -0.5)
    ffn_conv_b = nrm(ks[19], (DEPTH, D_FF), 0.02)
    w_ffn_down = nrm(ks[20], (DEPTH, D_FF, D_MODEL), D_FF ** -0.5 * beta)
    ln2_g = 1.0 + nrm(ks[21], (DEPTH, D_MODEL), 0.02)
    ln2_b = nrm(ks[22], (DEPTH, D_MODEL), 0.02)
    return {'x': x, 'w_in': w_in, 'b_in': b_in, 'mlstm_conv_w': mlstm_conv_w, 'mlstm_conv_b': mlstm_conv_b,
            'mlstm_norm_w': mlstm_norm_w, 'lambda_q1': lambda_q1, 'lambda_k1': lambda_k1, 'lambda_q2': lambda_q2,
            'lambda_k2': lambda_k2, 'diff_norm_w': diff_norm_w, 'rel_bias': rel_bias, 'w_branch_mlstm': w_branch_mlstm,
            'w_branch_diff': w_branch_diff, 'w_out': w_out, 'ln1_g': ln1_g, 'ln1_b': ln1_b, 'w_ffn_up': w_ffn_up,
            'ffn_conv_w': ffn_conv_w, 'ffn_conv_b': ffn_conv_b, 'w_ffn_down': w_ffn_down, 'ln2_g': ln2_g, 'ln2_b': ln2_b}


def reference(x, w_in, b_in, mlstm_conv_w, mlstm_conv_b, mlstm_norm_w, lambda_q1, lambda_k1, lambda_q2, lambda_k2,
              diff_norm_w, rel_bias, w_branch_mlstm, w_branch_diff, w_out, ln1_g, ln1_b, w_ffn_up, ffn_conv_w,
              ffn_conv_b, w_ffn_down, ln2_g, ln2_b):
    h = x
    for l in range(DEPTH):
        lam_init = 0.8 - 0.6 * math.exp(-0.3 * l)
        mix = token_mixer(h, w_in[l], b_in[l], mlstm_conv_w[l], mlstm_conv_b[l], mlstm_norm_w[l], lambda_q1[l],
                          lambda_k1[l], lambda_q2[l], lambda_k2[l], diff_norm_w[l], rel_bias, w_branch_mlstm[l],
                          w_branch_diff[l], w_out[l], lam_init)
        h = layer_norm(DEEPNORM_ALPHA * h + mix, ln1_g[l], ln1_b[l])
        ffn = conv_ffn(h, w_ffn_up[l], ffn_conv_w[l], ffn_conv_b[l], w_ffn_down[l])
        h = layer_norm(DEEPNORM_ALPHA * h + ffn, ln2_g[l], ln2_b[l])
    return h
```

```python
import math
import numpy as np
import concourse.bass as bass
import concourse.mybir as mybir
from concourse.bass_utils import run_bass_kernel_spmd

F32 = mybir.dt.float32
BF16 = mybir.dt.bfloat16
AF = mybir.ActivationFunctionType
ALU = mybir.AluOpType

S = 2048
D = 1024
NT = 16
NIN = 9224
DFF = 2816
NFC = 22
ALPHA = 2.0 ** 0.25
LAM_INIT = 0.8 - 0.6 * math.exp(0.0)
LN_EPS = 1e-5
GL = 2560
NEG = -30000.0

ENGS = ("pe", "act", "dve", "pool", "sp")


class Res:
    __slots__ = ("w", "r")

    def __init__(self):
        self.w = None
        self.r = []


class Op:
    __slots__ = ("eng", "fn", "deps", "eidx", "needed", "sig", "dma", "semkey", "prev_dma_val")

    def __init__(self, eng, fn, dma):
        self.eng = eng
        self.fn = fn
        self.deps = []
        self.dma = dma
        self.needed = dma
        self.sig = None
        self.semkey = None
        self.prev_dma_val = 0


class _Recorder:
    def __init__(self):
        self.call = None

    def __getattr__(self, name):
        def f(*args, **kwargs):
            self.call = (name, args, kwargs)
            return None
        return f


class Planner:
    NDSEM = 16

    def __init__(self, nc):
        self.nc = nc
        self.ops = {e: [] for e in ENGS}
        self.all = []
        self.dmas_since_barrier = []

    def op(self, eng, fn, reads=(), writes=(), dma=False, extra=()):
        if fn is not None:
            rec = _Recorder()
            fn(rec)
            assert rec.call is not None
            mname, args, kwargs = rec.call
            fn = (lambda e, mname=mname, args=args, kwargs=kwargs: getattr(e, mname)(*args, **kwargs))
        o = Op(eng, fn, dma)
        deps = []
        for r in reads:
            if r.w is not None:
                deps.append(r.w)
        for w in writes:
            if w.w is not None:
                deps.append(w.w)
            deps.extend(w.r)
        deps.extend(extra)
        for r in reads:
            if not dma:
                r.r = [x for x in r.r if x.dma or x.eng != eng]
            r.r.append(o)
        for w in writes:
            w.w = o
            w.r = []
        seen = set()
        for d in deps:
            if d is o or id(d) in seen:
                continue
            seen.add(id(d))
            o.deps.append(d)
        o.eidx = len(self.ops[eng])
        self.ops[eng].append(o)
        self.all.append(o)
        if dma:
            self.dmas_since_barrier.append(o)
        return o

    def dma(self, queue, out, in_, reads=(), writes=(), **kw):
        return self.op(queue, lambda e: e.dma_start(out=out, in_=in_, **kw), reads, writes, dma=True)

    def barrier(self):
        last = []
        for e in ENGS:
            for o in reversed(self.ops[e]):
                if not o.dma and o.fn is not None:
                    last.append(o)
                    break
        deps = last + list(self.dmas_since_barrier)
        self.dmas_since_barrier = []
        for e in ENGS:
            self.op(e, None, extra=deps)

    def finalize_and_emit(self):
        nc = self.nc
        for o in self.all:
            keep = []
            for d in o.deps:
                if d.fn is None:
                    continue
                if d.dma:
                    keep.append(d)
                elif d.eng != o.eng:
                    d.needed = True
                    keep.append(d)
                else:
                    if o.eng == "pe" and not o.dma:
                        continue
                    d.needed = True
                    keep.append(d)
            o.deps = keep
        cnt = {e: 0 for e in ENGS}
        dcnt = {e: [0] * self.NDSEM for e in ENGS}
        di = {e: 0 for e in ENGS}
        for o in self.all:
            if o.fn is None:
                continue
            if o.dma:
                q = o.eng
                o.semkey = ("d", q, di[q])
                o.prev_dma_val = dcnt[q][di[q]]
                dcnt[q][di[q]] += 16
                o.sig = dcnt[q][di[q]]
                di[q] = (di[q] + 1) % self.NDSEM
            elif o.needed:
                cnt[o.eng] += 1
                o.sig = cnt[o.eng]
                o.semkey = o.eng
        sems = {e: nc.alloc_semaphore("s_" + e) for e in ENGS}
        for q in ENGS:
            if any(o.dma for o in self.ops[q]):
                for i in range(self.NDSEM):
                    sems[("d", q, i)] = nc.alloc_semaphore("s_d%s%d" % (q, i))
        handles = {"pe": "tensor", "act": "scalar", "dve": "vector", "pool": "gpsimd", "sp": "sync"}
        with nc.Block() as block:
            for e in ENGS:
                ops = self.ops[e]
                if not ops:
                    continue

                def body(h, ops=ops):
                    seen = {}
                    for o in ops:
                        for d in o.deps:
                            if seen.get(d.semkey, 0) < d.sig:
                                h.wait_ge(sems[d.semkey], d.sig)
                                seen[d.semkey] = d.sig
                        if o.fn is None:
                            continue
                        if o.dma and o.prev_dma_val > 0:
                            if seen.get(o.semkey, 0) < o.prev_dma_val:
                                h.wait_ge(sems[o.semkey], o.prev_dma_val)
                                seen[o.semkey] = o.prev_dma_val
                        ins = o.fn(h)
                        if o.dma:
                            ins.then_inc(sems[o.semkey], 16)
                        elif o.needed:
                            ins.then_inc(sems[o.semkey], 1)

                getattr(block, handles[e])(body)


class Arena:
    def __init__(self, nc, words):
        self.t = nc.alloc_sbuf_tensor("arena", [128, words], F32)
        self.words = words

    def f32(self, off, n):
        assert off + n <= self.words, (off, n, self.words)
        return self.t[:, off:off + n]

    def bf16(self, off, n):
        assert n % 2 == 0 and off + n // 2 <= self.words, (off, n, self.words)
        return self.t[:, off:off + n // 2].bitcast(BF16)


class Bump:
    def __init__(self, arena, start, end):
        self.a = arena
        self.start = start
        self.end = end
        self.p = start

    def reset(self):
        self.p = self.start

    def f32(self, n):
        ap = self.a.f32(self.p, n)
        self.p += n
        assert self.p <= self.end, ("region overflow", self.p, self.end)
        return ap

    def bf16(self, n):
        n2 = (n + 1) // 2 * 2
        ap = self.a.bf16(self.p, n2)
        self.p += n2 // 2
        assert self.p <= self.end, ("region overflow", self.p, self.end)
        return ap[:, 0:n] if n2 != n else ap


def t5_bucket_np(dist):
    n = np.maximum(dist, 0)
    nf = np.maximum(n, 1).astype(np.float32)
    large = 16 + (np.log(nf / np.float32(16)) / np.float32(math.log(128 / 16)) * np.float32(16)).astype(np.int32)
    large = np.minimum(large, 31)
    return np.where(n < 16, n, large)


def onehot_table():
    oh = np.zeros((33, GL), dtype=np.float32)
    j = np.arange(GL)
    d = j - 511
    bk = t5_bucket_np(d)
    valid = d >= 0
    oh[bk[valid], j[valid]] = 1.0
    oh[32, ~valid] = NEG
    return oh


def build_program(debug=False):
    nc = bass.Bass("TRN2", target_bir_lowering=False)

    def din(name, shape):
        return nc.dram_tensor(name, list(shape), F32, kind="ExternalInput").ap()

    x = din("x", [S, D])
    w_in = din("w_in", [D, NIN])
    b_in = din("b_in", [1, NIN])
    mcw = din("mlstm_conv_w", [4, 2048])
    mcb = din("mlstm_conv_b", [1, 2048])
    mnw_d = din("mlstm_norm_w", [1, 1024])
    lq1 = din("lambda_q1", [1, 64])
    lk1 = din("lambda_k1", [1, 64])
    lq2 = din("lambda_q2", [1, 64])
    lk2 = din("lambda_k2", [1, 64])
    dnw_d = din("diff_norm_w", [1, 128])
    relb = din("rel_bias", [32, 8])
    w_bm = din("w_branch_mlstm", [D, D])
    w_bd = din("w_branch_diff", [D, D])
    w_out = din("w_out", [D, D])
    ln1g_d = din("ln1_g", [1, D])
    ln1b_d = din("ln1_b", [1, D])
    w_up = din("w_ffn_up", [D, 2 * DFF])
    fcw = din("ffn_conv_w", [3, DFF])
    fcb = din("ffn_conv_b", [1, DFF])
    w_dn = din("w_ffn_down", [DFF, D])
    ln2g_d = din("ln2_g", [1, D])
    ln2b_d = din("ln2_b", [1, D])
    oh_d = din("oh", [33, GL])
    out = nc.dram_tensor("out", [S, D], F32, kind="ExternalOutput").ap()
    h1s_t = nc.dram_tensor("h1s", [S, D], F32, kind=("ExternalOutput" if debug else "Internal"))
    h1s = h1s_t.ap()
    gext_t = nc.dram_tensor("gext", [8, GL], F32)
    gext = gext_t.ap()
    if debug:
        dbg_hm = nc.dram_tensor("dbg_hm", [128, 8 * S], F32, kind="ExternalOutput").ap()
        dbg_hd = nc.dram_tensor("dbg_hd", [128, 8 * S], F32, kind="ExternalOutput").ap()

    P = Planner(nc)
    A = Arena(nc, 51712)
    PB = [nc.alloc_psum_tensor("pb%d" % i, [128, 1024], F32) for i in range(4)]

    def bank(i):
        return PB[i // 2][:, (i % 2) * 512:(i % 2) * 512 + 512]

    def bank_bf(i):
        return bank(i).bitcast(BF16)

    rbank = [Res() for _ in range(8)]

    RP = Bump(A, 0, 1024)
    X0, HM0, HD0, W0 = 1024, 9216, 17408, 25600
    WEND = 51712
    xT = A.bf16(X0, 8 * S).rearrange("p (c t) -> p c t", c=8)
    hmT = A.bf16(HM0, 8 * S).rearrange("p (c t) -> p c t", c=8)
    hdT = A.bf16(HD0, 8 * S).rearrange("p (c t) -> p c t", c=8)
    r_xT, r_hmT, r_hdT = Res(), Res(), Res()
    RW = Bump(A, W0, WEND)

    ident = RP.bf16(128)
    identf = RP.f32(128)
    Jb = RP.bf16(128)
    cvA = RP.f32(128)
    cvB = RP.f32(88)
    cvq = RP.f32(8)
    neglam = RP.f32(1)
    lamtmp = RP.f32(8)
    sm = RP.f32(32)
    Jf = RP.f32(128)
    r_sm = Res()
    r_const = Res()
    r_cv = Res()
    r_lam = Res()

    P.op("pool", lambda e: e.memset(identf, 0.0), writes=[r_const])
    P.op("pool", lambda e: e.affine_select(out=identf, in_=identf, pattern=[[-1, 128]], compare_op=ALU.not_equal, fill=1.0, base=0, channel_multiplier=1), writes=[r_const])
    P.op("dve", lambda e: e.tensor_copy(out=ident, in_=identf), writes=[r_const])
    jtmp = RW.f32(128)
    P.op("pool", lambda e: e.memset(jtmp, 0.0), writes=[r_const])
    P.op("pool", lambda e: e.affine_select(out=jtmp, in_=jtmp, pattern=[[1, 128]], compare_op=ALU.not_equal, fill=1.0, base=-127, channel_multiplier=1), writes=[r_const])
    P.op("dve", lambda e: e.tensor_copy(out=Jb, in_=jtmp), writes=[r_const])
    P.op("dve", lambda e: e.tensor_copy(out=Jf, in_=jtmp), writes=[r_const])

    stA = RW.f32(128)
    stB = RW.f32(128)
    r_st = Res()

    def rows(ap1, a, b):
        return ap1[0:1, a:b].rearrange("o (r p) -> (o r) p", p=128)

    P.dma("sp", stA[0:16, :], rows(b_in, 0, 2048), writes=[r_st])
    P.dma("sp", stA[16:24, :], rows(b_in, 4104, 5128), writes=[r_st])
    P.dma("sp", stA[24:32, :], rows(b_in, 5128, 6152), writes=[r_st])
    P.dma("sp", stA[32:48, :], rows(b_in, 7176, 9224), writes=[r_st])
    for k in range(4):
        P.dma("sp", stA[48 + 16 * k:64 + 16 * k, :], rows(mcw[k:k + 1, :], 0, 2048), writes=[r_st])
    P.dma("sp", stA[112:128, :], rows(mcb, 0, 2048), writes=[r_st])
    for k in range(3):
        P.dma("sp", stB[22 * k:22 * k + 22, :], rows(fcw[k:k + 1, :], 0, DFF), writes=[r_st])
    P.dma("sp", stB[66:88, :], rows(fcb, 0, DFF), writes=[r_st])
    P.op("pe", lambda e: e.transpose(bank(5)[:, 0:128], stA, identf), reads=[r_st, r_const], writes=[rbank[5]])
    P.op("dve", lambda e: e.tensor_copy(out=cvA, in_=bank(5)[:, 0:128]), writes=[rbank[5], r_cv])
    P.op("pe", lambda e: e.transpose(bank(6)[:, 0:88], stB[0:88, :], identf[0:88, 0:88]), reads=[r_st, r_const], writes=[rbank[6]])
    P.op("dve", lambda e: e.tensor_copy(out=cvB, in_=bank(6)[:, 0:88]), writes=[rbank[6], r_cv])
    P.op("dve", lambda e: e.tensor_scalar(out=cvq, in0=cvA[:, 16:24], scalar1=0.125, scalar2=None, op0=ALU.mult), writes=[r_cv])

    lst = RW.f32(8)
    ones_f = RW.f32(128)
    P.op("pool", lambda e: e.memset(ones_f, 1.0), writes=[r_const])
    for i, v in enumerate((lq1, lk1, lq2, lk2)):
        P.dma("sp", lst[0:64, i:i + 1], v[0:1, :].rearrange("o (p k) -> (o p) k", k=1), writes=[r_lam])
    P.op("dve", lambda e: e.tensor_tensor(out=lst[0:64, 4:5], in0=lst[0:64, 0:1], in1=lst[0:64, 1:2], op=ALU.mult), writes=[r_lam])
    P.op("dve", lambda e: e.tensor_tensor(out=lst[0:64, 5:6], in0=lst[0:64, 2:3], in1=lst[0:64, 3:4], op=ALU.mult), writes=[r_lam])
    P.op("pe", lambda e: e.matmul(bank(7)[:, 0:2], ones_f[0:64, :], lst[0:64, 4:6], start=True, stop=True), reads=[r_lam, r_const], writes=[rbank[7]])
    P.op("act", lambda e: e.activation(out=lamtmp[:, 0:2], in_=bank(7)[:, 0:2], func=AF.Exp), writes=[rbank[7], r_lam])
    P.op("dve", lambda e: e.tensor_tensor(out=lamtmp[:, 2:3], in0=lamtmp[:, 1:2], in1=lamtmp[:, 0:1], op=ALU.subtract), writes=[r_lam])
    P.op("dve", lambda e: e.tensor_scalar(out=neglam, in0=lamtmp[:, 2:3], scalar1=-LAM_INIT, scalar2=None, op0=ALU.add), writes=[r_lam])

    rba = RW.f32(8)
    ohs = RW.f32(GL)
    gsb8 = RW.f32(GL)
    r_g = Res()
    r_gext = Res()
    P.op("pool", lambda e: e.memset(rba[0:33, :], 1.0), writes=[r_g])
    P.dma("sp", rba[0:32, :], relb, writes=[r_g])
    P.dma("sp", ohs[0:33, :], oh_d, writes=[r_g])
    for j in range(5):
        P.op("pe", lambda e, j=j: e.matmul(bank(4)[0:8, :], rba[0:33, :], ohs[0:33, j * 512:(j + 1) * 512], start=True, stop=True), reads=[r_g], writes=[rbank[4]])
        P.op("dve", lambda e, j=j: e.tensor_copy(out=gsb8[0:8, j * 512:(j + 1) * 512], in_=bank(4)[0:8, :]), writes=[rbank[4], r_g])
    P.dma("sp", gext, gsb8[0:8, :], reads=[r_g], writes=[r_gext])

    P.barrier()
    RW.reset()

    WSLOT = 4096
    wslots = [RW.bf16(WSLOT) for _ in range(3)]
    r_wslot = [Res() for _ in range(3)]
    wctr = [0]

    def wload(segs):
        i = wctr[0] % 3
        wctr[0] += 1
        tot = sum(s.shape[1] for s in segs)
        assert tot <= 512
        view = wslots[i][:, 0:8 * tot].rearrange("p (k c) -> p k c", k=8)
        c0 = 0
        for s in segs:
            n = s.shape[1]
            P.dma("pool", view[:, :, c0:c0 + n], s.rearrange("(k p) c -> p k c", p=128), writes=[r_wslot[i]])
            c0 += n
        return view, r_wslot[i]

    xb = [RW.bf16(1024) for _ in range(2)]
    r_xb = [Res(), Res()]
    for t in range(NT):
        i = t % 2
        P.dma("pool", xb[i], x[t * 128:(t + 1) * 128, :], writes=[r_xb[i]])
        pst = bank_bf(i)
        for kc in range(8):
            P.op("pe", lambda e, kc=kc, pst=pst, i=i: e.transpose(pst[:, kc * 128:(kc + 1) * 128], xb[i][:, kc * 128:(kc + 1) * 128], ident), reads=[r_xb[i], r_const], writes=[rbank[i]])
        eng = "act" if t % 2 == 0 else "dve"
        dst = xT[:, :, t * 128:(t + 1) * 128]
        src = pst.rearrange("p (c t) -> p c t", c=8)
        if eng == "act":
            P.op("act", lambda e, dst=dst, src=src: e.copy(out=dst, in_=src), writes=[rbank[i], r_xT])
        else:
            P.op("dve", lambda e, dst=dst, src=src: e.tensor_copy(out=dst, in_=src), writes=[rbank[i], r_xT])

    bg = RW.f32(8)
    gsb = RW.f32(128)
    e1 = RW.f32(64)
    nlf = RW.f32(64)
    t1 = RW.f32(64)
    t2 = RW.f32(64)
    wk = RW.f32(64)
    ws2 = RW.f32(64)
    einv = RW.f32(64)
    decay = RW.f32(64)
    Um = RW.f32(128)
    r_gs = Res()
    P.dma("sp", bg, b_in[0:1, 4096:4104].partition_broadcast(128), writes=[r_gs])
    P.op("pool", lambda e: e.memset(Um, 1.0), writes=[r_gs])
    P.op("pool", lambda e: e.affine_select(out=Um, in_=Um, pattern=[[1, 128]], compare_op=ALU.is_ge, fill=0.0, base=0, channel_multiplier=-1), writes=[r_gs])
    wg, r_wg = wload([w_in[:, 4096:4104]])
    for t in range(NT):
        for kc in range(8):
            P.op("pe", lambda e, t=t, kc=kc: e.matmul(bank(2)[:, t * 8:(t + 1) * 8], xT[:, kc, t * 128:(t + 1) * 128], wg[:, kc, :], start=(kc == 0), stop=(kc == 7)), reads=[r_xT, r_wg], writes=[rbank[2]])
    g3 = gsb.rearrange("p (t g) -> p t g", t=16)
    P.op("dve", lambda e: e.tensor_tensor(out=g3, in0=bank(2)[:, 0:128].rearrange("p (t g) -> p t g", t=16), in1=bg.unsqueeze(1).to_broadcast([128, 16, 8]), op=ALU.add), writes=[rbank[2], r_gs])
    v4 = lambda ap: ap.rearrange("p (t g) -> p t g", t=16)
    P.op("act", lambda e: e.activation(out=v4(e1), in_=g3[:, :, 4:8], func=AF.Exp, scale=-1.0), writes=[r_gs])
    P.op("act", lambda e: e.activation(out=nlf, in_=e1, func=AF.Ln, bias=1.0, scale=1.0), writes=[r_gs])
    P.op("pe", lambda e: e.matmul(bank(3)[:, 0:64], Um, nlf, start=True, stop=True), reads=[r_gs], writes=[rbank[3]])
    ones2 = RW.f32(128)
    P.op("pool", lambda e: e.memset(ones2, 1.0), writes=[r_gs])
    P.op("pe", lambda e: e.matmul(bank(3)[:, 64:128], ones2, nlf, start=True, stop=True), reads=[r_gs], writes=[rbank[3]])
    P.op("dve", lambda e: e.tensor_tensor(out=v4(t1), in0=g3[:, :, 0:4], in1=v4(bank(3)[:, 0:64]), op=ALU.add), writes=[rbank[3], r_gs])
    P.op("dve", lambda e: e.tensor_tensor(out=t2, in0=t1, in1=bank(3)[:, 64:128], op=ALU.subtract), writes=[rbank[3], r_gs])
    LN16 = math.log(16.0)
    P.op("act", lambda e: e.activation(out=wk, in_=t1, func=AF.Exp), writes=[r_gs])
    P.op("act", lambda e: e.activation(out=ws2, in_=t2, func=AF.Exp), writes=[r_gs])
    P.op("act", lambda e: e.activation(out=einv, in_=bank(3)[:, 0:64], func=AF.Exp), writes=[rbank[3], r_gs])
    P.op("act", lambda e: e.activation(out=decay, in_=bank(3)[:, 64:128], func=AF.Exp, scale=-1.0), writes=[rbank[3], r_gs])
    P.op("dve", lambda e: e.tensor_scalar(out=wk, in0=wk, scalar1=0.0625, scalar2=None, op0=ALU.mult), writes=[r_gs])
    P.op("dve", lambda e: e.tensor_scalar(out=ws2, in0=ws2, scalar1=0.0625, scalar2=None, op0=ALU.mult), writes=[r_gs])

    maskT = RW.f32(128)
    P.op("pool", lambda e: e.memset(maskT, 1.0), writes=[r_gs])
    P.op("pool", lambda e: e.affine_select(out=maskT, in_=maskT, pattern=[[1, 128]], compare_op=ALU.is_ge, fill=0.0, base=0, channel_multiplier=-1), writes=[r_gs])
    mnw = RW.f32(1024)
    r_rows = Res()
    P.dma("sp", mnw, mnw_d.partition_broadcast(128), writes=[r_rows])
    mark2 = RW.p
    RHD = Bump(A, HD0, W0)
    qk_s = [[RW.bf16(S) for _ in range(4)], [RHD.bf16(S) for _ in range(4)]]
    r_qk_s = [[Res() for _ in range(4)] for _ in range(2)]
    va_s = [RW.bf16(16 * 258).rearrange("p (t c) -> p t c", t=16), RHD.bf16(16 * 258).rearrange("p (t c) -> p t c", t=16)]
    r_va_s = [Res(), Res()]
    pre1 = RW.f32(S + 4)
    r_pre1 = Res()
    acc = RW.f32(S)
    r_acc = Res()
    C32 = RW.f32(2 * 257).rearrange("p (a b) -> p a b", a=2)
    Cb = RW.bf16(2 * 258).rearrange("p (a b) -> p a b", a=2)
    r_C32, r_Cb = Res(), Res()
    sqk_sb = [RW.bf16(128) for _ in range(2)]
    kw_sb = [RW.bf16(256) for _ in range(2)]
    r_sqk = [Res(), Res()]
    r_kw = [Res(), Res()]
    numS = [RW.f32(258) for _ in range(3)]
    ogb = [RW.f32(256) for _ in range(2)]
    ogl = RW.f32(256)
    xx = [RW.f32(256) for _ in range(2)]
    hnn = RW.f32(256)
    hmb = [RW.bf16(256) for _ in range(2)]
    smc = [RW.f32(32) for _ in range(2)]
    r_numS = [Res(), Res(), Res()]
    r_ogb = [Res(), Res()]
    r_ogl = Res()
    r_xx = [Res(), Res()]
    r_hnn = Res()
    r_hmb = [Res(), Res()]
    r_smc = [Res(), Res()]
    bo_bf = RW.bf16(1024)
    bv_bf = RW.bf16(1024)
    ones_row = RW.bf16(128)
    mhalf2 = RW.f32(2)
    r_bo = Res()
    P.op("pool", lambda e: e.memset(mhalf2, -0.5), writes=[r_bo])
    P.dma("pool", bo_bf[0:1, :], b_in[0:1, 3072:4096], writes=[r_bo])
    P.dma("pool", bv_bf[0:1, :], b_in[0:1, 2048:3072], writes=[r_bo])
    P.op("pool", lambda e: e.memset(ones_row[0:1, :], 1.0), writes=[r_bo])
    P.op("pool", lambda e: e.memset(pre1[:, 0:4], 0.0), writes=[r_pre1])
    for si in range(2):
        P.op("pool", lambda e: e.memset(va_s[si][:, :, 256:258], 1.0), writes=[r_va_s[si]])

    def mlstm_wload(h):
        a_ = wload([w_in[:, h * 256:(h + 1) * 256], w_in[:, 1024 + h * 256:1024 + (h + 1) * 256]])
        b_ = wload([w_in[:, 2048 + h * 256:2048 + (h + 1) * 256], w_in[:, 3072 + h * 256:3072 + (h + 1) * 256]])
        return a_, b_

    def proj_ops(h, si, wA, r_wA, wB, r_wB):
        L = []

        def qk_mm(c, tc):
            bq = 0 if tc % 2 == 0 else 2

            def f():
                for kc in range(8):
                    P.op("pe", lambda e: e.matmul(bank(bq), wA[:, kc, c * 128:(c + 1) * 128], xT[:, kc, tc * 512:(tc + 1) * 512], start=(kc == 0), stop=(kc == 7)), reads=[r_wA, r_xT], writes=[rbank[bq]])
            return f

        def qk_ev(c, tc, cc):
            bq = 0 if tc % 2 == 0 else 2
            return lambda: P.op("act", lambda e: e.activation(out=pre1[:, 3 + tc * 512:3 + (tc + 1) * 512], in_=bank(bq), func=AF.Identity, bias=cvA[:, cc:cc + 1], scale=1.0), reads=[r_cv], writes=[rbank[bq], r_pre1])

        def conv_ops(c, cc):
            cwc = lambda k: cvA[:, 48 + 16 * k + cc:48 + 16 * k + cc + 1]
            return [
                lambda: P.op("act", lambda e: e.activation(out=acc, in_=pre1[:, 3:3 + S], func=AF.Identity, scale=cwc(3), bias=cvA[:, 112 + cc:113 + cc]), reads=[r_pre1, r_cv], writes=[r_acc]),
                lambda: P.op("dve", lambda e: e.scalar_tensor_tensor(out=acc[:, 0:1024], in0=pre1[:, 2:2 + 1024], scalar=cwc(2), in1=acc[:, 0:1024], op0=ALU.mult, op1=ALU.add), reads=[r_pre1, r_cv], writes=[r_acc]),
                lambda: P.op("dve", lambda e: e.scalar_tensor_tensor(out=acc[:, 1024:S], in0=pre1[:, 2 + 1024:2 + S], scalar=cwc(2), in1=acc[:, 1024:S], op0=ALU.mult, op1=ALU.add), reads=[r_pre1, r_cv], writes=[r_acc]),
                lambda: P.op("dve", lambda e: e.scalar_tensor_tensor(out=acc[:, 0:1024], in0=pre1[:, 1:1 + 1024], scalar=cwc(1), in1=acc[:, 0:1024], op0=ALU.mult, op1=ALU.add), reads=[r_pre1, r_cv], writes=[r_acc]),
                lambda: P.op("dve", lambda e: e.scalar_tensor_tensor(out=acc[:, 1024:S], in0=pre1[:, 1 + 1024:1 + S], scalar=cwc(1), in1=acc[:, 1024:S], op0=ALU.mult, op1=ALU.add), reads=[r_pre1, r_cv], writes=[r_acc]),
                lambda: P.op("dve", lambda e: e.scalar_tensor_tensor(out=acc[:, 0:1024], in0=pre1[:, 0:1024], scalar=cwc(0), in1=acc[:, 0:1024], op0=ALU.mult, op1=ALU.add), reads=[r_pre1, r_cv], writes=[r_acc]),
                lambda: P.op("dve", lambda e: e.scalar_tensor_tensor(out=acc[:, 1024:S], in0=pre1[:, 1024:S], scalar=cwc(0), in1=acc[:, 1024:S], op0=ALU.mult, op1=ALU.add), reads=[r_pre1, r_cv], writes=[r_acc]),
                lambda: P.op("act", lambda e: e.activation(out=qk_s[si][c], in_=acc, func=AF.Silu), reads=[r_acc], writes=[r_qk_s[si][c]]),
            ]

        def v_mm(t):
            bv = 2 if t % 2 == 0 else 0

            def f():
                for kc in range(8):
                    P.op("pe", lambda e: e.matmul(bank(bv)[:, 0:256], xT[:, kc, t * 128:(t + 1) * 128], wB[:, kc, 0:256], start=(kc == 0), stop=False), reads=[r_wB, r_xT], writes=[rbank[bv]])
                P.op("pe", lambda e: e.matmul(bank(bv)[:, 0:256], ones_row[0:1, :], bv_bf[0:1, h * 256:(h + 1) * 256], start=False, stop=True), reads=[r_bo], writes=[rbank[bv]])
            return f

        def v_ev(t):
            bv = 2 if t % 2 == 0 else 0
            return lambda: P.op("dve", lambda e: e.tensor_copy(out=va_s[si][:, t, 0:256], in_=bank(bv)[:, 0:256]), writes=[rbank[bv], r_va_s[si]])

        for c in range(4):
            cc = (h * 2 + c) if c < 2 else (8 + h * 2 + (c - 2))
            SPC = lambda: None
            L.append(qk_mm(c, 0))
            for tc in range(4):
                if tc + 1 < 4:
                    L.append(qk_mm(c, tc + 1))
                else:
                    L.append(SPC)
                L.append(qk_ev(c, tc, cc))
            co = conv_ops(c, cc)
            L.extend([SPC, SPC, co[0], SPC, SPC])
            L.extend(co[1:-1])
            L.extend([SPC] * 5)
            L.append(co[-1])
            L.append(v_mm(4 * c))
            for t in range(4 * c, 4 * c + 4):
                if t + 1 < 4 * c + 4:
                    L.append(v_mm(t + 1))
                L.append(v_ev(t))
        return L

    def chunk_loop(h, si, wB, r_wB, bg):
        q0, q1, k0, k1 = qk_s[si]
        rq0, rq1, rk0, rk1 = r_qk_s[si]
        vaug = va_s[si]
        r_va = r_va_s[si]

        def drip(n=1):
            for _ in range(n):
                if bg:
                    bg.pop(0)()

        def stage1(t):
            ts = slice(t * 128, (t + 1) * 128)
            col = t * 4 + h
            i = t % 2
            P.op("pe", lambda e: e.matmul(bank(3)[:, 0:128], k0[:, ts], q0[:, ts], start=True, stop=False), reads=[rk0, rq0], writes=[rbank[3]])
            P.op("pe", lambda e: e.matmul(bank(3)[:, 0:128], k1[:, ts], q1[:, ts], start=False, stop=True), reads=[rk1, rq1], writes=[rbank[3]])
            P.op("dve", lambda e: e.scalar_tensor_tensor(out=sqk_sb[i], in0=bank(3)[:, 0:128], scalar=wk[:, col:col + 1], in1=maskT, op0=ALU.mult, op1=ALU.mult), reads=[r_gs], writes=[rbank[3], r_sqk[i]])
            kps = bank_bf(4)
            P.op("pe", lambda e: e.transpose(kps[:, 0:128], k0[:, ts], ident), reads=[rk0, r_const], writes=[rbank[4]])
            P.op("pe", lambda e: e.transpose(kps[:, 128:256], k1[:, ts], ident), reads=[rk1, r_const], writes=[rbank[4]])
            P.op("act", lambda e: e.activation(out=kw_sb[i], in_=kps[:, 0:256], func=AF.Copy, scale=ws2[:, col:col + 1]), reads=[r_gs], writes=[rbank[4], r_kw[i]])
            if t >= 1:
                stage2a_pe(t - 1)
            for dc in range(2):
                P.op("pe", lambda e: e.matmul(PB[3][:, dc * 512:dc * 512 + 257], kw_sb[i][:, dc * 128:(dc + 1) * 128], vaug[:, t, 0:257], start=True, stop=True), reads=[r_kw[i], r_va], writes=[rbank[6 + dc]])
            if t > 0:
                P.op("pe", lambda e: e.matmul(bank(5)[:, 0:257], q0[:, ts], Cb[:, 0, 0:257], start=True, stop=False), reads=[rq0, r_Cb], writes=[rbank[5]])
                P.op("pe", lambda e: e.matmul(bank(5)[:, 0:257], q1[:, ts], Cb[:, 1, 0:257], start=False, stop=False), reads=[rq1, r_Cb], writes=[rbank[5]])
            P.op("pe", lambda e: e.matmul(bank(5)[:, 0:257], sqk_sb[i], vaug[:, t, 0:257], start=(t == 0), stop=True), reads=[r_sqk[i], r_va], writes=[rbank[5]])

        def stage1b(t):
            col = t * 4 + h
            dC = PB[3][:, :].rearrange("p (a b) -> p a b", a=2)[:, :, 0:257]
            if t == 0:
                P.op("dve", lambda e: e.tensor_copy(out=C32, in_=dC), writes=[rbank[6], rbank[7], r_C32])
            else:
                P.op("dve", lambda e: e.scalar_tensor_tensor(out=C32, in0=C32, scalar=decay[:, col:col + 1], in1=dC, op0=ALU.mult, op1=ALU.add), reads=[r_gs], writes=[rbank[6], rbank[7], r_C32])
            if t < NT - 1:
                P.op("act", lambda e: e.copy(out=Cb[:, :, 0:257], in_=C32), reads=[r_C32], writes=[r_Cb])
            P.op("act", lambda e: e.copy(out=numS[t % 3][:, 0:257], in_=bank(5)[:, 0:257]), writes=[rbank[5], r_numS[t % 3]])

        def stage2a_pe(t):
            ts = slice(t * 128, (t + 1) * 128)
            for kc in range(8):
                P.op("pe", lambda e: e.matmul(bank(1)[:, 0:256], xT[:, kc, ts], wB[:, kc, 256:512], start=(kc == 0), stop=False), reads=[r_wB, r_xT], writes=[rbank[1]])
            P.op("pe", lambda e: e.matmul(bank(1)[:, 0:256], ones_row[0:1, :], bo_bf[0:1, h * 256:(h + 1) * 256], start=False, stop=True), reads=[r_bo], writes=[rbank[1]])

        def stage2a(t):
            i = t % 2
            if t == NT - 1:
                stage2a_pe(t)
            P.op("act", lambda e: e.activation(out=ogb[i], in_=bank(1)[:, 0:256], func=AF.Tanh, scale=0.5), writes=[rbank[1], r_ogb[i]])

        def stage2b(t):
            col = t * 4 + h
            i = t % 2
            sm_ = smc[i]
            rs_ = r_smc[i]
            ns_ = numS[t % 3]
            rns_ = r_numS[t % 3]
            den = ns_[:, 256:257]
            P.op("dve", lambda e: e.scalar_tensor_tensor(out=sm_[:, 0:1], in0=den, scalar=-1.0, in1=den, op0=ALU.mult, op1=ALU.max), reads=[rns_], writes=[rs_])
            P.op("dve", lambda e: e.tensor_scalar(out=sm_[:, 1:2], in0=sm_[:, 0:1], scalar1=einv[:, col:col + 1], scalar2=None, op0=ALU.max), reads=[r_gs], writes=[rs_])
            P.op("dve", lambda e: e.scalar_tensor_tensor(out=sm_[:, 2:3], in0=sm_[:, 1:2], scalar=4.0 * LN_EPS, in1=sm_[:, 1:2], op0=ALU.mult, op1=ALU.mult), writes=[rs_])
            P.op("dve", lambda e: e.scalar_tensor_tensor(out=xx[i], in0=ogb[i], scalar=1.0, in1=ns_[:, 0:256], op0=ALU.add, op1=ALU.mult), reads=[rns_, r_ogb[i]], writes=[r_xx[i]])
            P.op("dve", lambda e: e.bn_stats(out=sm_[:, 8:14], in_=xx[i]), reads=[r_xx[i]], writes=[rs_])
            P.op("dve", lambda e: e.bn_aggr(out=sm_[:, 14:16], in_=sm_[:, 8:14]), writes=[rs_])
            P.op("dve", lambda e: e.tensor_tensor(out=sm_[:, 16:17], in0=sm_[:, 15:16], in1=sm_[:, 2:3], op=ALU.add), writes=[rs_])

        def stage2b_act(t):
            i = t % 2
            sm_ = smc[i]
            rs_ = r_smc[i]
            P.op("pool", lambda e: e.tensor_tensor(out=sm_[:, 17:18], in0=sm_[:, 16:17], in1=mhalf2[:, 0:1], op=ALU.pow), reads=[r_bo], writes=[rs_])

        def stage2c(t):
            ts = slice(t * 128, (t + 1) * 128)
            i = t % 2
            sm_ = smc[i]
            rs_ = r_smc[i]
            P.op("dve", lambda e: e.tensor_scalar(out=hnn, in0=xx[i], scalar1=sm_[:, 14:15], scalar2=sm_[:, 17:18], op0=ALU.subtract, op1=ALU.mult), reads=[r_xx[i], rs_], writes=[r_hnn])
            P.op("dve", lambda e: e.tensor_tensor(out=hmb[i], in0=hnn, in1=mnw[:, h * 256:(h + 1) * 256], op=ALU.mult), reads=[r_hnn, r_rows], writes=[r_hmb[i]])

        def stage2c_pe(t):
            ts = slice(t * 128, (t + 1) * 128)
            i = t % 2
            tps = bank_bf(1)
            for dc in range(2):
                P.op("pe", lambda e: e.transpose(tps[:, dc * 128:(dc + 1) * 128], hmb[i][:, dc * 128:(dc + 1) * 128], ident), reads=[r_hmb[i], r_const], writes=[rbank[1]])
            P.op("dve", lambda e: e.tensor_copy(out=hmT[:, 2 * h:2 * h + 2, ts], in_=tps[:, 0:256].rearrange("p (a b) -> p a b", a=2)), writes=[rbank[1], r_hmT])

        for t in range(NT + 5):
            if 0 <= t - 5 < NT:
                stage2c_pe(t - 5)
            if t < NT:
                stage1(t)
                drip()
            if 0 <= t - 1 < NT:
                stage2a(t - 1)
                drip()
            if t < NT:
                stage1b(t)
                drip()
            if 0 <= t - 2 < NT:
                stage2b(t - 2)
                drip()
            if 0 <= t - 3 < NT:
                stage2c(t - 3)
                drip()
            if 0 <= t - 2 < NT:
                stage2b_act(t - 2)
                drip()
        while bg:
            bg.pop(0)()

    wl = mlstm_wload(0)
    for f_ in proj_ops(0, 0, wl[0][0], wl[0][1], wl[1][0], wl[1][1]):
        f_()
    for h in range(4):
        si = h % 2
        cur_wB, cur_rwB = wl[1]
        bg = []
        if h + 1 < 4:
            wl = mlstm_wload(h + 1)
            bg = proj_ops(h + 1, 1 - si, wl[0][0], wl[0][1], wl[1][0], wl[1][1])
        chunk_loop(h, si, cur_wB, cur_rwB, bg)

    P.barrier()
    RW.p = mark2

    bvd = RW.f32(1024)
    dnwc = RW.f32(2)
    mhalf = RW.f32(2)
    ones_bf = RW.bf16(128)
    ones_ff = RW.f32(128)
    P.dma("sp", bvd, b_in[0:1, 6152:7176].partition_broadcast(128), writes=[r_rows])
    P.dma("sp", dnwc[:, 0:1], dnw_d[0:1, :].rearrange("o (p k) -> (o p) k", k=1), writes=[r_rows])
    P.op("dve", lambda e: e.tensor_scalar(out=dnwc[:, 1:2], in0=dnwc[:, 0:1], scalar1=(1.0 - LAM_INIT), scalar2=None, op0=ALU.mult), writes=[r_rows])
    P.op("pool", lambda e: e.memset(mhalf, -0.5), writes=[r_rows])
    P.op("pool", lambda e: e.memset(ones_ff, 1.0), writes=[r_rows])
    P.op("dve", lambda e: e.tensor_copy(out=ones_bf, in_=ones_ff), writes=[r_rows])
    aq = RW.bf16(S)
    ak = [RW.bf16(S), RW.bf16(S)]
    r_aq, r_ak = Res(), Res()
    av = RW.bf16(16 * 256).rearrange("p (t c) -> p t c", t=16)
    r_av = Res()
    HW = 2432
    Hb1 = RW.f32(HW)
    Hb = [Hb1, Hb1]
    r_H1 = Res()
    r_H = [r_H1, r_H1]
    NE = 4
    Eb = [RW.bf16(512) for _ in range(NE)]
    r_E = [Res() for _ in range(NE)]
    E0b = [RW.bf16(512) for _ in range(NE)]
    r_E0 = [Res() for _ in range(NE)]
    EBT = [RW.bf16(HW) for _ in range(2)]
    r_EBT = [Res(), Res()]
    fin_1 = [RW.f32(512) for _ in range(4)]
    fin_t = [fin_1, fin_1]
    r_fin1 = [Res() for _ in range(4)]
    r_fin = [r_fin1, r_fin1]
    sq_t = RW.f32(512)
    rs_t = RW.f32(512)
    r_sq, r_rs = Res(), Res()
    P.op("pool", lambda e: e.memset(ak[0][64:128, :], 0.0), writes=[r_ak])
    P.op("pool", lambda e: e.memset(ak[1][0:64, :], 0.0), writes=[r_ak])

    def attn_wload(h):
        segs = [w_in[:, 4104 + h * 128:4104 + (h + 1) * 128], w_in[:, 5128 + h * 128:5128 + (h + 1) * 128]]
        if h % 2 == 0:
            segs.append(w_in[:, 6152 + h * 128:6152 + (h + 2) * 128])
        return wload(segs)

    def attn_hload(h):
        P.dma("sp", Hb[h % 2], bass.AP(gext_t, h * GL, [[1, 128], [1, HW]]), reads=[r_gext], writes=[r_H[h % 2]])

    def build_ebt(h):
        Hh_ = Hb[h % 2]
        rH_ = r_H[h % 2]
        P.op("act", lambda e: e.activation(out=Hh_, in_=Hh_, func=AF.Exp), writes=[rH_])
        for j in range(5):
            c0_, c1_ = j * 512, min(HW, (j + 1) * 512)
            bi_ = j % 2
            P.op("pe", lambda e: e.matmul(bank(bi_)[:, 0:c1_ - c0_], Jf, Hh_[:, c0_:c1_], start=True, stop=True), reads=[rH_, r_const], writes=[rbank[bi_]])
            if j % 2 == 0:
                P.op("dve", lambda e: e.tensor_copy(out=EBT[h % 2][:, c0_:c1_], in_=bank(bi_)[:, 0:c1_ - c0_]), writes=[rbank[bi_], r_EBT[h % 2]])
            else:
                P.op("act", lambda e: e.copy(out=EBT[h % 2][:, c0_:c1_], in_=bank(bi_)[:, 0:c1_ - c0_]), writes=[rbank[bi_], r_EBT[h % 2]])

    nextw = attn_wload(0)
    attn_hload(0)
    fctr = 0
    for h in range(8):
        wD, r_wD = nextw
        build_ebt(h)
        EBh = EBT[h % 2]
        rEB = r_EBT[h % 2]
        for tc in range(4):
            tsl = slice(tc * 512, (tc + 1) * 512)
            for kc in range(8):
                P.op("pe", lambda e: e.matmul(bank(0), wD[:, kc, 0:128], xT[:, kc, tsl], start=(kc == 0), stop=(kc == 7)), reads=[r_wD, r_xT], writes=[rbank[0]])
            P.op("act", lambda e: e.activation(out=aq[:, tsl], in_=bank(0), func=AF.Identity, bias=cvq[:, h:h + 1], scale=0.125), reads=[r_cv], writes=[rbank[0], r_aq])
            for kc in range(8):
                P.op("pe", lambda e: e.matmul(bank(1), wD[:, kc, 128:256], xT[:, kc, tsl], start=(kc == 0), stop=(kc == 7)), reads=[r_wD, r_xT], writes=[rbank[1]])
            P.op("dve", lambda e: e.tensor_scalar(out=ak[0][0:64, tsl], in0=bank(1)[0:64, :], scalar1=cvA[0:64, 24 + h:25 + h], scalar2=None, op0=ALU.add), reads=[r_cv], writes=[rbank[1], r_ak])
            P.op("dve", lambda e: e.tensor_scalar(out=ak[1][64:128, tsl], in0=bank(1)[64:128, :], scalar1=cvA[64:128, 24 + h:25 + h], scalar2=None, op0=ALU.add), reads=[r_cv], writes=[rbank[1], r_ak])
        if h % 2 == 0:
            for t in range(NT):
                bi = 2 + (t % 2)
                for kc in range(8):
                    P.op("pe", lambda e: e.matmul(bank(bi)[:, 0:256], xT[:, kc, t * 128:(t + 1) * 128], wD[:, kc, 256:512], start=(kc == 0), stop=(kc == 7)), reads=[r_wD, r_xT], writes=[rbank[bi]])
                P.op("dve", lambda e: e.tensor_tensor(out=av[:, t, :], in0=bank(bi)[:, 0:256], in1=bvd[:, h * 128:(h + 2) * 128], op=ALU.add), reads=[r_rows], writes=[rbank[bi], r_av])
        vo = (h % 2) * 128
        if h + 1 < 8:
            nextw = attn_wload(h + 1)
            attn_hload(h + 1)
        steps = [(qc, kt, st) for qc in range(4) for kt in range(4 * qc + 4) for st in range(2)]
        deferred = []
        since_fin = 0
        free_bank = [0]
        n = len(steps)
        LOOK = 3

        def QK(i):
            qc, kt, st = steps[i]
            bi = i % 4
            c0 = max(0, kt * 128 - qc * 512)
            P.op("pe", lambda e: e.matmul(bank(bi)[:, c0:512], ak[st][:, kt * 128:(kt + 1) * 128], aq[:, qc * 512 + c0:(qc + 1) * 512], start=True, stop=True), reads=[r_ak, r_aq], writes=[rbank[bi]])

        for i in range(-LOOK, n):
            if i + LOOK < n:
                QK(i + LOOK)
            if i < 0:
                continue
            qc, kt, st = steps[i]
            bi = i % 4
            ei = i % NE
            last = 4 * qc + 3
            free_bank[0] = bi
            c0 = max(0, kt * 128 - qc * 512)
            m0 = qc * 512 - kt * 128 + 384
            P.op("act", lambda e: e.activation(out=E0b[ei][:, c0:512], in_=bank(bi)[:, c0:512], func=AF.Exp), writes=[rbank[bi], r_E0[ei]])
            P.op("dve", lambda e: e.tensor_tensor(out=Eb[ei][:, c0:512], in0=E0b[ei][:, c0:512], in1=EBh[:, m0 + c0:m0 + 512], op=ALU.mult), reads=[r_E0[ei], rEB], writes=[r_E[ei]])
            P.op("pe", lambda e: e.matmul(bank(4 + st)[:, c0:512], av[:, kt, vo:vo + 128], Eb[ei][:, c0:512], start=(kt == 0), stop=(kt == last)), reads=[r_av, r_E[ei]], writes=[rbank[4 + st]])
            P.op("pe", lambda e: e.matmul(bank(6 + st)[:, c0:512], ones_bf, Eb[ei][:, c0:512], start=(kt == 0), stop=(kt == last)), reads=[r_rows, r_E[ei]], writes=[rbank[6 + st]])
            if kt == last and st == 1:
                o1s, o2s, z1s, z2s = fin_t[0]
                ro1, ro2, rz1, rz2 = r_fin[0]
                assert not deferred
                P.op("act", lambda e: e.copy(out=o1s, in_=bank(4)), writes=[rbank[4], ro1])
                P.op("dve", lambda e: e.tensor_copy(out=z1s, in_=bank(6)), writes=[rbank[6], rz1])
                P.op("act", lambda e: e.copy(out=o2s, in_=bank(5)), writes=[rbank[5], ro2])
                P.op("dve", lambda e: e.tensor_copy(out=z2s, in_=bank(7)), writes=[rbank[7], rz2])
                hsl = hdT[:, h, qc * 512:(qc + 1) * 512]
                DQ = deferred.append
                DQ(lambda: P.op("act", lambda e: e.activation(out=z2s, in_=z2s, func=AF.Ln), writes=[rz2]))
                DQ(lambda: P.op("act", lambda e: e.activation(out=z2s, in_=z2s, func=AF.Exp, scale=-1.0), writes=[rz2]))
                DQ(lambda: P.op("dve", lambda e: e.scalar_tensor_tensor(out=z2s, in0=z2s, scalar=neglam, in1=z1s, op0=ALU.mult, op1=ALU.mult), reads=[rz1, r_lam], writes=[rz2]))
                DQ(lambda: P.op("dve", lambda e: e.tensor_tensor(out=o2s, in0=o2s, in1=z2s, op=ALU.mult), reads=[rz2], writes=[ro2]))
                DQ(lambda: P.op("dve", lambda e: e.tensor_tensor(out=o1s, in0=o1s, in1=o2s, op=ALU.add), reads=[ro2], writes=[ro1]))
                DQ(lambda: P.op("pool", lambda e: e.tensor_tensor(out=sq_t, in0=o1s, in1=o1s, op=ALU.mult), reads=[ro1], writes=[r_sq]))
                DQ(lambda: P.op("dve", lambda e: e.scalar_tensor_tensor(out=z1s, in0=z1s, scalar=LN_EPS, in1=z1s, op0=ALU.mult, op1=ALU.mult), writes=[rz1]))
                def ss_ops():
                    fb = free_bank[0]
                    P.op("pe", lambda e: e.matmul(bank(fb), ones_ff, sq_t, start=True, stop=True), reads=[r_sq, r_rows], writes=[rbank[fb]])
                    P.op("dve", lambda e: e.scalar_tensor_tensor(out=rs_t, in0=bank(fb), scalar=1.0 / 128.0, in1=z1s, op0=ALU.mult, op1=ALU.add), reads=[rz1], writes=[rbank[fb], r_rs])
                DQ(ss_ops)
                DQ(lambda: P.op("act", lambda e: e.activation(out=rs_t, in_=rs_t, func=AF.Ln), writes=[r_rs]))
                DQ(lambda: P.op("act", lambda e: e.activation(out=rs_t, in_=rs_t, func=AF.Exp, scale=-0.5), writes=[r_rs]))
                DQ(lambda: P.op("dve", lambda e: e.scalar_tensor_tensor(out=hsl, in0=o1s, scalar=dnwc[:, 1:2], in1=rs_t, op0=ALU.mult, op1=ALU.mult), reads=[ro1, r_rs, r_rows], writes=[r_hdT]))
                since_fin = 0
            else:
                since_fin += 1
                if since_fin > 2:
                    for _ in range(2):
                        if deferred:
                            deferred.pop(0)()
        while deferred:
            deferred.pop(0)()

    if debug:
        P.dma("pool", dbg_hm, hmT.rearrange("p c t -> p (c t)"), reads=[r_hmT])
        P.dma("pool", dbg_hd, hdT.rearrange("p c t -> p (c t)"), reads=[r_hdT])
    P.barrier()
    RW.reset()

    mergedT = RW.bf16(8 * S).rearrange("p (c t) -> p c t", c=8)
    r_mg = Res()
    mark4 = RW.p
    wslots[:] = [RW.bf16(WSLOT) for _ in range(3)]
    gmt = [RW.f32(512) for _ in range(2)]
    gdt = [RW.f32(512) for _ in range(2)]
    m1t = [RW.f32(512) for _ in range(2)]
    m2t = [RW.f32(512) for _ in range(2)]
    r_gm = [Res(), Res()]
    r_gd = [Res(), Res()]
    r_m1 = [Res(), Res()]
    r_m2 = [Res(), Res()]
    rr = 0
    def merge_wload(c):
        cs = slice(c * 128, (c + 1) * 128)
        return wload([w_bm[:, cs], w_bd[:, cs], w_in[:, 7176 + c * 128:7176 + (c + 1) * 128], w_in[:, 8200 + c * 128:8200 + (c + 1) * 128]])

    next_m = merge_wload(0)
    for c in range(8):
        wM, r_wM = next_m
        if c + 1 < 8:
            next_m = merge_wload(c + 1)
        for tc in range(4):
            b0 = (rr % 2) * 4
            i2 = rr % 2
            rr += 1
            tsl = slice(tc * 512, (tc + 1) * 512)
            srcs = [(hmT, r_hmT), (hdT, r_hdT), (xT, r_xT), (xT, r_xT)]
            for g in range(4):
                src, rs = srcs[g]
                for kc in range(8):
                    P.op("pe", lambda e, g=g, kc=kc, src=src, b0=b0, tsl=tsl: e.matmul(bank(b0 + g), wM[:, kc, g * 128:(g + 1) * 128], src[:, kc, tsl], start=(kc == 0), stop=(kc == 7)), reads=[r_wM, rs], writes=[rbank[b0 + g]])
            P.op("act", lambda e, b0=b0, i2=i2, c=c: e.activation(out=gmt[i2], in_=bank(b0 + 2), func=AF.Sigmoid, bias=cvA[:, 32 + c:33 + c], scale=1.0), reads=[r_cv], writes=[rbank[b0 + 2], r_gm[i2]])
            P.op("act", lambda e, b0=b0, i2=i2, c=c: e.activation(out=gdt[i2], in_=bank(b0 + 3), func=AF.Sigmoid, bias=cvA[:, 40 + c:41 + c], scale=1.0), reads=[r_cv], writes=[rbank[b0 + 3], r_gd[i2]])
            P.op("dve", lambda e, b0=b0, i2=i2: e.tensor_tensor(out=m1t[i2], in0=bank(b0), in1=gmt[i2], op=ALU.mult), reads=[r_gm[i2]], writes=[rbank[b0], r_m1[i2]])
            P.op("dve", lambda e, b0=b0, i2=i2: e.tensor_tensor(out=m2t[i2], in0=bank(b0 + 1), in1=gdt[i2], op=ALU.mult), reads=[r_gd[i2]], writes=[rbank[b0 + 1], r_m2[i2]])
            P.op("pool", lambda e, i2=i2, c=c, tsl=tsl: e.tensor_tensor(out=mergedT[:, c, tsl], in0=m1t[i2], in1=m2t[i2], op=ALU.add), reads=[r_m1[i2], r_m2[i2]], writes=[r_mg])

    P.barrier()
    RH = Bump(A, HM0, W0)
    wdn = RH.bf16(NFC * 1024).rearrange("p (f c) -> p f c", f=NFC)
    r_wdn = Res()
    wup = [RH.bf16(8 * 256).rearrange("p (k c) -> p k c", k=8) for _ in range(3)]
    r_wup = [Res() for _ in range(3)]

    def load_wup(f):
        i = f % 3
        P.dma("pool", wup[i][:, :, 0:128], w_up[:, f * 128:(f + 1) * 128].rearrange("(k p) c -> p k c", p=128), writes=[r_wup[i]])
        P.dma("pool", wup[i][:, :, 128:256], w_up[:, DFF + f * 128:DFF + (f + 1) * 128].rearrange("(k p) c -> p k c", p=128), writes=[r_wup[i]])

    RW.p = mark4
    h1T = xT
    r_h1T = Res()
    wo = RW.bf16(8 * 1024).rearrange("p (k c) -> p k c", k=8)
    r_wo = Res()
    for hh in range(2):
        P.dma("pool", wo[:, :, hh * 512:(hh + 1) * 512], w_out[:, hh * 512:(hh + 1) * 512].rearrange("(k p) c -> p k c", p=128), writes=[r_wo])
    load_wup(0)
    for f in range(NFC):
        P.dma("pool", wdn[:, f, :], w_dn[f * 128:(f + 1) * 128, :], writes=[r_wdn])
    lng = RW.f32(1024)
    lnb = RW.f32(1024)
    r_ln = Res()
    P.dma("sp", lng, ln1g_d.partition_broadcast(128), writes=[r_ln])
    P.dma("sp", lnb, ln1b_d.partition_broadcast(128), writes=[r_ln])
    NB4 = 3
    xres = [RW.f32(1024) for _ in range(NB4)]
    yt = [RW.f32(1024) for _ in range(NB4)]
    hbt = [RW.bf16(1024) for _ in range(NB4)]
    lsm = [RW.f32(32) for _ in range(NB4)]
    r_xres = [Res() for _ in range(NB4)]
    r_yt = [Res() for _ in range(NB4)]
    r_hbt = [Res() for _ in range(NB4)]
    r_lsm = [Res() for _ in range(NB4)]

    def interleave(lists, skew):
        n_ = len(lists)
        pos = [0] * n_
        step = 0
        while any(pos[k] < len(lists[k]) for k in range(n_)):
            for k in range(n_):
                if step >= k * skew and pos[k] < len(lists[k]):
                    lists[k][pos[k]]()
                    pos[k] += 1
            step += 1

    def ln_ops(y_, s_, r_y, r_s, xr_, r_xr, ps_ap, ps_res, g_row, b_row, r_rowsln):
        L = []
        L.append(lambda: P.op("dve", lambda e: e.scalar_tensor_tensor(out=y_, in0=xr_, scalar=ALPHA, in1=ps_ap, op0=ALU.mult, op1=ALU.add), reads=[r_xr], writes=list(ps_res) + [r_y]))
        L.append(lambda: P.op("dve", lambda e: e.bn_stats(out=s_[:, 0:6], in_=y_[:, 0:512]), reads=[r_y], writes=[r_s]))
        L.append(lambda: P.op("dve", lambda e: e.bn_stats(out=s_[:, 6:12], in_=y_[:, 512:1024]), reads=[r_y], writes=[r_s]))
        L.append(lambda: P.op("dve", lambda e: e.bn_aggr(out=s_[:, 12:14], in_=s_[:, 0:12]), writes=[r_s]))
        L.append(lambda: P.op("act", lambda e: e.activation(out=s_[:, 14:15], in_=s_[:, 13:14], func=AF.Ln, bias=LN_EPS, scale=1.0), writes=[r_s]))
        L.append(lambda: P.op("act", lambda e: e.activation(out=s_[:, 15:16], in_=s_[:, 14:15], func=AF.Exp, scale=-0.5), writes=[r_s]))
        L.append(lambda: P.op("dve", lambda e: e.scalar_tensor_tensor(out=s_[:, 16:17], in0=s_[:, 12:13], scalar=-1.0, in1=s_[:, 15:16], op0=ALU.mult, op1=ALU.mult), writes=[r_s]))
        L.append(lambda: P.op("act", lambda e: e.activation(out=y_, in_=y_, func=AF.Identity, scale=s_[:, 15:16], bias=s_[:, 16:17]), reads=[r_s], writes=[r_y]))
        L.append(lambda: P.op("dve", lambda e: e.tensor_tensor(out=y_, in0=y_, in1=g_row, op=ALU.mult), reads=[r_rowsln], writes=[r_y]))
        L.append(lambda: P.op("dve", lambda e: e.tensor_tensor(out=y_, in0=y_, in1=b_row, op=ALU.add), reads=[r_rowsln], writes=[r_y]))
        return L

    out_dmas = []

    def tile4b(t):
        i = t % NB4
        ts = slice(t * 128, (t + 1) * 128)
        pb = PB[i]
        L = []
        L.append(lambda: P.dma("sp", xres[i], x[ts, :], writes=[r_xres[i]]))

        def mm(hh):
            for kc in range(8):
                P.op("pe", lambda e: e.matmul(pb[:, hh * 512:(hh + 1) * 512], mergedT[:, kc, ts], wo[:, kc, hh * 512:(hh + 1) * 512], start=(kc == 0), stop=(kc == 7)), reads=[r_mg, r_wo], writes=[rbank[2 * i + hh]])
        L.append(lambda: mm(0))
        L.append(lambda: mm(1))
        L.extend(ln_ops(yt[i], lsm[i], r_yt[i], r_lsm[i], xres[i], r_xres[i], pb[:, :], [rbank[2 * i], rbank[2 * i + 1]], lng, lnb, r_ln))
        L.append(lambda: P.dma("sp", h1s[ts, :], yt[i], reads=[r_yt[i]]))
        L.append(lambda: P.op("act", lambda e: e.copy(out=hbt[i], in_=yt[i]), reads=[r_yt[i]], writes=[r_hbt[i]]))
        tb = 6 + (t % 2)
        tps = bank_bf(tb)

        def trs():
            for kc in range(8):
                P.op("pe", lambda e: e.transpose(tps[:, kc * 128:(kc + 1) * 128], hbt[i][:, kc * 128:(kc + 1) * 128], ident), reads=[r_hbt[i], r_const], writes=[rbank[tb]])
        L.append(trs)
        L.append(lambda: P.op("dve", lambda e: e.tensor_copy(out=h1T[:, :, ts], in_=tps.rearrange("p (c t) -> p c t", c=8)), writes=[rbank[tb], r_h1T]))
        return L

    interleave([tile4b(t) for t in range(NT)], 6)

    P.barrier()

    RW.reset()
    gT = RW.bf16(NFC * 1024).rearrange("p (f t) -> p f t", f=NFC)
    r_gT = Res()
    lng2 = RW.f32(1024)
    lnb2 = RW.f32(1024)
    r_ln2 = Res()
    P.dma("sp", lng2, ln2g_d.partition_broadcast(128), writes=[r_ln2])
    P.dma("sp", lnb2, ln2b_d.partition_broadcast(128), writes=[r_ln2])
    xres5 = [RW.f32(1024) for _ in range(3)]
    yt5 = [RW.f32(1024) for _ in range(3)]
    lsm5 = [RW.f32(32) for _ in range(3)]
    r_xres5 = [Res() for _ in range(3)]
    r_yt5 = [Res() for _ in range(3)]
    r_lsm5 = [Res() for _ in range(3)]
    abuf = [RW.f32(1028) for _ in range(2)]
    ybuf = [RW.f32(1024) for _ in range(2)]
    sbuf_ = [RW.f32(1024) for _ in range(2)]
    r_ab = [Res(), Res()]
    r_yb = [Res(), Res()]
    r_sb = [Res(), Res()]
    halo = RW.f32(NFC * 2).rearrange("p (f k) -> p f k", f=NFC)
    r_halo = Res()
    for i in range(2):
        P.op("pool", lambda e, i=i: e.memset(abuf[i][:, 0:2], 0.0), writes=[r_ab[i]])

    for hf in range(2):
        if hf == 1:
            load_wup(0)
        for f in range(NFC):
            if f + 1 < NFC:
                load_wup(f + 1)
            wi = f % 3
            ai = f % 2
            bb0 = 2 + 2 * (f % 2)
            if hf == 1:
                P.op("dve", lambda e, ai=ai, f=f: e.tensor_copy(out=abuf[ai][:, 0:2], in_=halo[:, f, :]), reads=[r_halo], writes=[r_ab[ai]])
            for tc2 in range(2):
                tsl = slice(hf * 1024 + tc2 * 512, hf * 1024 + (tc2 + 1) * 512)
                for kc in range(8):
                    P.op("pe", lambda e, kc=kc, wi=wi, tc2=tc2, tsl=tsl: e.matmul(bank(tc2), wup[wi][:, kc, 0:128], h1T[:, kc, tsl], start=(kc == 0), stop=(kc == 7)), reads=[r_wup[wi], r_h1T], writes=[rbank[tc2]])
                P.op("act", lambda e, tc2=tc2, ai=ai: e.copy(out=abuf[ai][:, 2 + tc2 * 512:2 + (tc2 + 1) * 512], in_=bank(tc2)), writes=[rbank[tc2], r_ab[ai]])
                for kc in range(8):
                    P.op("pe", lambda e, kc=kc, wi=wi, tc2=tc2, tsl=tsl, bb0=bb0: e.matmul(bank(bb0 + tc2), wup[wi][:, kc, 128:256], h1T[:, kc, tsl], start=(kc == 0), stop=(kc == 7)), reads=[r_wup[wi], r_h1T], writes=[rbank[bb0 + tc2]])
            fw = lambda k, f=f: cvB[:, 22 * k + f:22 * k + f + 1]
            P.op("act", lambda e, ai=ai, f=f, fw=fw: e.activation(out=ybuf[ai], in_=abuf[ai][:, 2:1026], func=AF.Identity, scale=fw(2), bias=cvB[:, 66 + f:67 + f]), reads=[r_ab[ai], r_cv], writes=[r_yb[ai]])
            P.op("dve", lambda e, ai=ai, fw=fw: e.scalar_tensor_tensor(out=ybuf[ai], in0=abuf[ai][:, 1:1025], scalar=fw(1), in1=ybuf[ai], op0=ALU.mult, op1=ALU.add), reads=[r_ab[ai], r_cv], writes=[r_yb[ai]])
            P.op("dve", lambda e, ai=ai, fw=fw: e.scalar_tensor_tensor(out=ybuf[ai], in0=abuf[ai][:, 0:1024], scalar=fw(0), in1=ybuf[ai], op0=ALU.mult, op1=ALU.add), reads=[r_ab[ai], r_cv], writes=[r_yb[ai]])
            if hf == 0:
                P.op("pool", lambda e, ai=ai, f=f: e.tensor_copy(out=halo[:, f, :], in_=abuf[ai][:, 1024:1026]), reads=[r_ab[ai]], writes=[r_halo])
            P.op("act", lambda e, ai=ai: e.activation(out=sbuf_[ai], in_=ybuf[ai], func=AF.Silu), reads=[r_yb[ai]], writes=[r_sb[ai]])
            for tc2 in range(2):
                P.op("dve", lambda e, ai=ai, tc2=tc2, f=f, bb0=bb0: e.tensor_tensor(out=gT[:, f, tc2 * 512:(tc2 + 1) * 512], in0=bank(bb0 + tc2), in1=sbuf_[ai][:, tc2 * 512:(tc2 + 1) * 512], op=ALU.mult), reads=[r_sb[ai]], writes=[rbank[bb0 + tc2], r_gT])
        def tile5(tl, hf=hf):
            t = hf * 8 + tl
            i = t % 3
            ts = slice(t * 128, (t + 1) * 128)
            tls = slice(tl * 128, (tl + 1) * 128)
            pb = PB[i]
            pres = [rbank[2 * i], rbank[2 * i + 1]]
            L = []
            L.append(lambda: P.dma("sp", xres5[i], h1s[ts, :], writes=[r_xres5[i]]))

            def mm(hh):
                for f in range(NFC):
                    P.op("pe", lambda e: e.matmul(pb[:, hh * 512:(hh + 1) * 512], gT[:, f, tls], wdn[:, f, hh * 512:(hh + 1) * 512], start=(f == 0), stop=(f == NFC - 1)), reads=[r_gT, r_wdn], writes=[pres[hh]])
            L.append(lambda: mm(0))
            L.append(lambda: mm(1))
            L.extend(ln_ops(yt5[i], lsm5[i], r_yt5[i], r_lsm5[i], xres5[i], r_xres5[i], pb[:, :], pres, lng2, lnb2, r_ln2))
            L.append(lambda: out_dmas.append(P.dma("sp", out[ts, :], yt5[i], reads=[r_yt5[i]])))
            return L

        interleave([tile5(tl) for tl in range(8)], 5)

    P.op("sp", None, extra=out_dmas + list(P.dmas_since_barrier))
    P.finalize_and_emit()
    return nc


_NC_CACHE = {}


def _get_nc(debug=False):
    if debug not in _NC_CACHE:
        _NC_CACHE[debug] = build_program(debug)
    return _NC_CACHE[debug]


def kernel(x, w_in, b_in, mlstm_conv_w, mlstm_conv_b, mlstm_norm_w, lambda_q1, lambda_k1, lambda_q2, lambda_k2,
           diff_norm_w, rel_bias, w_branch_mlstm, w_branch_diff, w_out, ln1_g, ln1_b, w_ffn_up, ffn_conv_w,
           ffn_conv_b, w_ffn_down, ln2_g, ln2_b, _debug=False):
    f = lambda a: np.ascontiguousarray(np.asarray(a, dtype=np.float32))
    x = f(x)
    shared = {
        "w_in": f(w_in)[0], "b_in": f(b_in), "mlstm_conv_w": f(mlstm_conv_w)[0], "mlstm_conv_b": f(mlstm_conv_b),
        "mlstm_norm_w": f(mlstm_norm_w), "lambda_q1": f(lambda_q1), "lambda_k1": f(lambda_k1),
        "lambda_q2": f(lambda_q2), "lambda_k2": f(lambda_k2), "diff_norm_w": f(diff_norm_w),
        "rel_bias": f(rel_bias), "w_branch_mlstm": f(w_branch_mlstm)[0], "w_branch_diff": f(w_branch_diff)[0],
        "w_out": f(w_out)[0], "ln1_g": f(ln1_g), "ln1_b": f(ln1_b), "w_ffn_up": f(w_ffn_up)[0],
        "ffn_conv_w": f(ffn_conv_w)[0], "ffn_conv_b": f(ffn_conv_b), "w_ffn_down": f(w_ffn_down)[0],
        "ln2_g": f(ln2_g), "ln2_b": f(ln2_b), "oh": onehot_table(),
    }
    nc = _get_nc(_debug)
    ncores = x.shape[0]
    in_maps = [dict(shared, x=x[b]) for b in range(ncores)]
    res = run_bass_kernel_spmd(nc, in_maps, core_ids=list(range(ncores)))
    if _debug:
        return res
    return np.stack([r["out"] for r in res.results], axis=0)
```

```python
import math
import numpy as np
import concourse.bass as bass
import concourse.mybir as mybir
from concourse.bass_utils import run_bass_kernel_spmd

F32 = mybir.dt.float32
BF16 = mybir.dt.bfloat16
AF = mybir.ActivationFunctionType
ALU = mybir.AluOpType

S = 2048
D = 1024
NT = 16
NIN = 9224
DFF = 2816
NFC = 22
ALPHA = 2.0 ** 0.25
LAM_INIT = 0.8 - 0.6 * math.exp(0.0)
LN_EPS = 1e-5
GL = 2560
NEG = -30000.0

ENGS = ("pe", "act", "dve", "pool", "sp")


class Res:
    __slots__ = ("w", "r")

    def __init__(self):
        self.w = None
        self.r = []


class Op:
    __slots__ = ("eng", "fn", "deps", "eidx", "needed", "sig", "dma", "semkey", "prev_dma_val")

    def __init__(self, eng, fn, dma):
        self.eng = eng
        self.fn = fn
        self.deps = []
        self.dma = dma
        self.needed = dma
        self.sig = None
        self.semkey = None
        self.prev_dma_val = 0


class _Recorder:
    def __init__(self):
        self.call = None

    def __getattr__(self, name):
        def f(*args, **kwargs):
            self.call = (name, args, kwargs)
            return None
        return f


class Planner:
    NDSEM = 16

    def __init__(self, nc):
        self.nc = nc
        self.ops = {e: [] for e in ENGS}
        self.all = []
        self.dmas_since_barrier = []

    def op(self, eng, fn, reads=(), writes=(), dma=False, extra=()):
        if fn is not None:
            rec = _Recorder()
            fn(rec)
            assert rec.call is not None
            mname, args, kwargs = rec.call
            fn = (lambda e, mname=mname, args=args, kwargs=kwargs: getattr(e, mname)(*args, **kwargs))
        o = Op(eng, fn, dma)
        deps = []
        for r in reads:
            if r.w is not None:
                deps.append(r.w)
        for w in writes:
            if w.w is not None:
                deps.append(w.w)
            deps.extend(w.r)
        deps.extend(extra)
        for r in reads:
            if not dma:
                r.r = [x for x in r.r if x.dma or x.eng != eng]
            r.r.append(o)
        for w in writes:
            w.w = o
            w.r = []
        seen = set()
        for d in deps:
            if d is o or id(d) in seen:
                continue
            seen.add(id(d))
            o.deps.append(d)
        o.eidx = len(self.ops[eng])
        self.ops[eng].append(o)
        self.all.append(o)
        if dma:
            self.dmas_since_barrier.append(o)
        return o

    def dma(self, queue, out, in_, reads=(), writes=(), **kw):
        return self.op(queue, lambda e: e.dma_start(out=out, in_=in_, **kw), reads, writes, dma=True)

    def barrier(self):
        last = []
        for e in ENGS:
            for o in reversed(self.ops[e]):
                if not o.dma and o.fn is not None:
                    last.append(o)
                    break
        deps = last + list(self.dmas_since_barrier)
        self.dmas_since_barrier = []
        for e in ENGS:
            self.op(e, None, extra=deps)

    def finalize_and_emit(self):
        nc = self.nc
        for o in self.all:
            keep = []
            for d in o.deps:
                if d.fn is None:
                    continue
                if d.dma:
                    keep.append(d)
                elif d.eng != o.eng:
                    d.needed = True
                    keep.append(d)
                else:
                    if o.eng == "pe" and not o.dma:
                        continue
                    d.needed = True
                    keep.append(d)
            o.deps = keep
        cnt = {e: 0 for e in ENGS}
        dcnt = {e: [0] * self.NDSEM for e in ENGS}
        di = {e: 0 for e in ENGS}
        for o in self.all:
            if o.fn is None:
                continue
            if o.dma:
                q = o.eng
                o.semkey = ("d", q, di[q])
                o.prev_dma_val = dcnt[q][di[q]]
                dcnt[q][di[q]] += 16
                o.sig = dcnt[q][di[q]]
                di[q] = (di[q] + 1) % self.NDSEM
            elif o.needed:
                cnt[o.eng] += 1
                o.sig = cnt[o.eng]
                o.semkey = o.eng
        sems = {e: nc.alloc_semaphore("s_" + e) for e in ENGS}
        for q in ENGS:
            if any(o.dma for o in self.ops[q]):
                for i in range(self.NDSEM):
                    sems[("d", q, i)] = nc.alloc_semaphore("s_d%s%d" % (q, i))
        handles = {"pe": "tensor", "act": "scalar", "dve": "vector", "pool": "gpsimd", "sp": "sync"}
        with nc.Block() as block:
            for e in ENGS:
                ops = self.ops[e]
                if not ops:
                    continue

                def body(h, ops=ops):
                    seen = {}
                    for o in ops:
                        for d in o.deps:
                            if seen.get(d.semkey, 0) < d.sig:
                                h.wait_ge(sems[d.semkey], d.sig)
                                seen[d.semkey] = d.sig
                        if o.fn is None:
                            continue
                        if o.dma and o.prev_dma_val > 0:
                            if seen.get(o.semkey, 0) < o.prev_dma_val:
                                h.wait_ge(sems[o.semkey], o.prev_dma_val)
                                seen[o.semkey] = o.prev_dma_val
                        ins = o.fn(h)
                        if o.dma:
                            ins.then_inc(sems[o.semkey], 16)
                        elif o.needed:
                            ins.then_inc(sems[o.semkey], 1)

                getattr(block, handles[e])(body)


class Arena:
    def __init__(self, nc, words):
        self.t = nc.alloc_sbuf_tensor("arena", [128, words], F32)
        self.words = words

    def f32(self, off, n):
        assert off + n <= self.words, (off, n, self.words)
        return self.t[:, off:off + n]

    def bf16(self, off, n):
        assert n % 2 == 0 and off + n // 2 <= self.words, (off, n, self.words)
        return self.t[:, off:off + n // 2].bitcast(BF16)


class Bump:
    def __init__(self, arena, start, end):
        self.a = arena
        self.start = start
        self.end = end
        self.p = start

    def reset(self):
        self.p = self.start

    def f32(self, n):
        ap = self.a.f32(self.p, n)
        self.p += n
        assert self.p <= self.end, ("region overflow", self.p, self.end)
        return ap

    def bf16(self, n):
        n2 = (n + 1) // 2 * 2
        ap = self.a.bf16(self.p, n2)
        self.p += n2 // 2
        assert self.p <= self.end, ("region overflow", self.p, self.end)
        return ap[:, 0:n] if n2 != n else ap


def t5_bucket_np(dist):
    n = np.maximum(dist, 0)
    nf = np.maximum(n, 1).astype(np.float32)
    large = 16 + (np.log(nf / np.float32(16)) / np.float32(math.log(128 / 16)) * np.float32(16)).astype(np.int32)
    large = np.minimum(large, 31)
    return np.where(n < 16, n, large)


def onehot_table():
    oh = np.zeros((33, GL), dtype=np.float32)
    j = np.arange(GL)
    d = j - 511
    bk = t5_bucket_np(d)
    valid = d >= 0
    oh[bk[valid], j[valid]] = 1.0
    oh[32, ~valid] = NEG
    return oh


def build_program(debug=False):
    nc = bass.Bass("TRN2", target_bir_lowering=False)

    def din(name, shape):
        return nc.dram_tensor(name, list(shape), F32, kind="ExternalInput").ap()

    x = din("x", [S, D])
    w_in = din("w_in", [D, NIN])
    b_in = din("b_in", [1, NIN])
    mcw = din("mlstm_conv_w", [4, 2048])
    mcb = din("mlstm_conv_b", [1, 2048])
    mnw_d = din("mlstm_norm_w", [1, 1024])
    lq1 = din("lambda_q1", [1, 64])
    lk1 = din("lambda_k1", [1, 64])
    lq2 = din("lambda_q2", [1, 64])
    lk2 = din("lambda_k2", [1, 64])
    dnw_d = din("diff_norm_w", [1, 128])
    relb = din("rel_bias", [32, 8])
    w_bm = din("w_branch_mlstm", [D, D])
    w_bd = din("w_branch_diff", [D, D])
    w_out = din("w_out", [D, D])
    ln1g_d = din("ln1_g", [1, D])
    ln1b_d = din("ln1_b", [1, D])
    w_up = din("w_ffn_up", [D, 2 * DFF])
    fcw = din("ffn_conv_w", [3, DFF])
    fcb = din("ffn_conv_b", [1, DFF])
    w_dn = din("w_ffn_down", [DFF, D])
    ln2g_d = din("ln2_g", [1, D])
    ln2b_d = din("ln2_b", [1, D])
    oh_d = din("oh", [33, GL])
    out = nc.dram_tensor("out", [S, D], F32, kind="ExternalOutput").ap()
    h1s_t = nc.dram_tensor("h1s", [S, D], F32, kind=("ExternalOutput" if debug else "Internal"))
    h1s = h1s_t.ap()
    gext_t = nc.dram_tensor("gext", [8, GL], F32)
    gext = gext_t.ap()
    if debug:
        dbg_hm = nc.dram_tensor("dbg_hm", [128, 8 * S], F32, kind="ExternalOutput").ap()
        dbg_hd = nc.dram_tensor("dbg_hd", [128, 8 * S], F32, kind="ExternalOutput").ap()

    P = Planner(nc)
    A = Arena(nc, 51712)
    PB = [nc.alloc_psum_tensor("pb%d" % i, [128, 1024], F32) for i in range(4)]

    def bank(i):
        return PB[i // 2][:, (i % 2) * 512:(i % 2) * 512 + 512]

    def bank_bf(i):
        return bank(i).bitcast(BF16)

    rbank = [Res() for _ in range(8)]

    RP = Bump(A, 0, 1024)
    X0, HM0, HD0, W0 = 1024, 9216, 17408, 25600
    WEND = 51712
    xT = A.bf16(X0, 8 * S).rearrange("p (c t) -> p c t", c=8)
    hmT = A.bf16(HM0, 8 * S).rearrange("p (c t) -> p c t", c=8)
    hdT = A.bf16(HD0, 8 * S).rearrange("p (c t) -> p c t", c=8)
    r_xT, r_hmT, r_hdT = Res(), Res(), Res()
    RW = Bump(A, W0, WEND)

    ident = RP.bf16(128)
    identf = RP.f32(128)
    Jb = RP.bf16(128)
    cvA = RP.f32(128)
    cvB = RP.f32(88)
    cvq = RP.f32(8)
    neglam = RP.f32(1)
    lamtmp = RP.f32(8)
    sm = RP.f32(32)
    Jf = RP.f32(128)
    r_sm = Res()
    r_const = Res()
    r_cv = Res()
    r_lam = Res()

    P.op("pool", lambda e: e.memset(identf, 0.0), writes=[r_const])
    P.op("pool", lambda e: e.affine_select(out=identf, in_=identf, pattern=[[-1, 128]], compare_op=ALU.not_equal, fill=1.0, base=0, channel_multiplier=1), writes=[r_const])
    P.op("dve", lambda e: e.tensor_copy(out=ident, in_=identf), writes=[r_const])
    jtmp = RW.f32(128)
    P.op("pool", lambda e: e.memset(jtmp, 0.0), writes=[r_const])
    P.op("pool", lambda e: e.affine_select(out=jtmp, in_=jtmp, pattern=[[1, 128]], compare_op=ALU.not_equal, fill=1.0, base=-127, channel_multiplier=1), writes=[r_const])
    P.op("dve", lambda e: e.tensor_copy(out=Jb, in_=jtmp), writes=[r_const])
    P.op("dve", lambda e: e.tensor_copy(out=Jf, in_=jtmp), writes=[r_const])

    stA = RW.f32(128)
    stB = RW.f32(128)
    r_st = Res()

    def rows(ap1, a, b):
        return ap1[0:1, a:b].rearrange("o (r p) -> (o r) p", p=128)

    P.dma("sp", stA[0:16, :], rows(b_in, 0, 2048), writes=[r_st])
    P.dma("sp", stA[16:24, :], rows(b_in, 4104, 5128), writes=[r_st])
    P.dma("sp", stA[24:32, :], rows(b_in, 5128, 6152), writes=[r_st])
    P.dma("sp", stA[32:48, :], rows(b_in, 7176, 9224), writes=[r_st])
    for k in range(4):
        P.dma("sp", stA[48 + 16 * k:64 + 16 * k, :], rows(mcw[k:k + 1, :], 0, 2048), writes=[r_st])
    P.dma("sp", stA[112:128, :], rows(mcb, 0, 2048), writes=[r_st])
    for k in range(3):
        P.dma("sp", stB[22 * k:22 * k + 22, :], rows(fcw[k:k + 1, :], 0, DFF), writes=[r_st])
    P.dma("sp", stB[66:88, :], rows(fcb, 0, DFF), writes=[r_st])
    P.op("pe", lambda e: e.transpose(bank(5)[:, 0:128], stA, identf), reads=[r_st, r_const], writes=[rbank[5]])
    P.op("dve", lambda e: e.tensor_copy(out=cvA, in_=bank(5)[:, 0:128]), writes=[rbank[5], r_cv])
    P.op("pe", lambda e: e.transpose(bank(6)[:, 0:88], stB[0:88, :], identf[0:88, 0:88]), reads=[r_st, r_const], writes=[rbank[6]])
    P.op("dve", lambda e: e.tensor_copy(out=cvB, in_=bank(6)[:, 0:88]), writes=[rbank[6], r_cv])
    P.op("dve", lambda e: e.tensor_scalar(out=cvq, in0=cvA[:, 16:24], scalar1=0.125, scalar2=None, op0=ALU.mult), writes=[r_cv])

    lst = RW.f32(8)
    ones_f = RW.f32(128)
    P.op("pool", lambda e: e.memset(ones_f, 1.0), writes=[r_const])
    for i, v in enumerate((lq1, lk1, lq2, lk2)):
        P.dma("sp", lst[0:64, i:i + 1], v[0:1, :].rearrange("o (p k) -> (o p) k", k=1), writes=[r_lam])
    P.op("dve", lambda e: e.tensor_tensor(out=lst[0:64, 4:5], in0=lst[0:64, 0:1], in1=lst[0:64, 1:2], op=ALU.mult), writes=[r_lam])
    P.op("dve", lambda e: e.tensor_tensor(out=lst[0:64, 5:6], in0=lst[0:64, 2:3], in1=lst[0:64, 3:4], op=ALU.mult), writes=[r_lam])
    P.op("pe", lambda e: e.matmul(bank(7)[:, 0:2], ones_f[0:64, :], lst[0:64, 4:6], start=True, stop=True), reads=[r_lam, r_const], writes=[rbank[7]])
    P.op("act", lambda e: e.activation(out=lamtmp[:, 0:2], in_=bank(7)[:, 0:2], func=AF.Exp), writes=[rbank[7], r_lam])
    P.op("dve", lambda e: e.tensor_tensor(out=lamtmp[:, 2:3], in0=lamtmp[:, 1:2], in1=lamtmp[:, 0:1], op=ALU.subtract), writes=[r_lam])
    P.op("dve", lambda e: e.tensor_scalar(out=neglam, in0=lamtmp[:, 2:3], scalar1=-LAM_INIT, scalar2=None, op0=ALU.add), writes=[r_lam])

    rba = RW.f32(8)
    ohs = RW.f32(GL)
    gsb8 = RW.f32(GL)
    r_g = Res()
    r_gext = Res()
    P.op("pool", lambda e: e.memset(rba[0:33, :], 1.0), writes=[r_g])
    P.dma("sp", rba[0:32, :], relb, writes=[r_g])
    P.dma("sp", ohs[0:33, :], oh_d, writes=[r_g])
    for j in range(5):
        P.op("pe", lambda e, j=j: e.matmul(bank(4)[0:8, :], rba[0:33, :], ohs[0:33, j * 512:(j + 1) * 512], start=True, stop=True), reads=[r_g], writes=[rbank[4]])
        P.op("dve", lambda e, j=j: e.tensor_copy(out=gsb8[0:8, j * 512:(j + 1) * 512], in_=bank(4)[0:8, :]), writes=[rbank[4], r_g])
    P.dma("sp", gext, gsb8[0:8, :], reads=[r_g], writes=[r_gext])

    P.barrier()
    RW.reset()

    WSLOT = 4096
    wslots = [RW.bf16(WSLOT) for _ in range(3)]
    r_wslot = [Res() for _ in range(3)]
    wctr = [0]

    def wload(segs):
        i = wctr[0] % 3
        wctr[0] += 1
        tot = sum(s.shape[1] for s in segs)
        assert tot <= 512
        view = wslots[i][:, 0:8 * tot].rearrange("p (k c) -> p k c", k=8)
        c0 = 0
        for s in segs:
            n = s.shape[1]
            P.dma("pool", view[:, :, c0:c0 + n], s.rearrange("(k p) c -> p k c", p=128), writes=[r_wslot[i]])
            c0 += n
        return view, r_wslot[i]

    xb = [RW.bf16(1024) for _ in range(2)]
    r_xb = [Res(), Res()]
    for t in range(NT):
        i = t % 2
        P.dma("pool", xb[i], x[t * 128:(t + 1) * 128, :], writes=[r_xb[i]])
        pst = bank_bf(i)
        for kc in range(8):
            P.op("pe", lambda e, kc=kc, pst=pst, i=i: e.transpose(pst[:, kc * 128:(kc + 1) * 128], xb[i][:, kc * 128:(kc + 1) * 128], ident), reads=[r_xb[i], r_const], writes=[rbank[i]])
        eng = "act" if t % 2 == 0 else "dve"
        dst = xT[:, :, t * 128:(t + 1) * 128]
        src = pst.rearrange("p (c t) -> p c t", c=8)
        if eng == "act":
            P.op("act", lambda e, dst=dst, src=src: e.copy(out=dst, in_=src), writes=[rbank[i], r_xT])
        else:
            P.op("dve", lambda e, dst=dst, src=src: e.tensor_copy(out=dst, in_=src), writes=[rbank[i], r_xT])

    bg = RW.f32(8)
    gsb = RW.f32(128)
    e1 = RW.f32(64)
    nlf = RW.f32(64)
    t1 = RW.f32(64)
    t2 = RW.f32(64)
    wk = RW.f32(64)
    ws2 = RW.f32(64)
    einv = RW.f32(64)
    decay = RW.f32(64)
    Um = RW.f32(128)
    r_gs = Res()
    P.dma("sp", bg, b_in[0:1, 4096:4104].partition_broadcast(128), writes=[r_gs])
    P.op("pool", lambda e: e.memset(Um, 1.0), writes=[r_gs])
    P.op("pool", lambda e: e.affine_select(out=Um, in_=Um, pattern=[[1, 128]], compare_op=ALU.is_ge, fill=0.0, base=0, channel_multiplier=-1), writes=[r_gs])
    wg, r_wg = wload([w_in[:, 4096:4104]])
    for t in range(NT):
        for kc in range(8):
            P.op("pe", lambda e, t=t, kc=kc: e.matmul(bank(2)[:, t * 8:(t + 1) * 8], xT[:, kc, t * 128:(t + 1) * 128], wg[:, kc, :], start=(kc == 0), stop=(kc == 7)), reads=[r_xT, r_wg], writes=[rbank[2]])
    g3 = gsb.rearrange("p (t g) -> p t g", t=16)
    P.op("dve", lambda e: e.tensor_tensor(out=g3, in0=bank(2)[:, 0:128].rearrange("p (t g) -> p t g", t=16), in1=bg.unsqueeze(1).to_broadcast([128, 16, 8]), op=ALU.add), writes=[rbank[2], r_gs])
    v4 = lambda ap: ap.rearrange("p (t g) -> p t g", t=16)
    P.op("act", lambda e: e.activation(out=v4(e1), in_=g3[:, :, 4:8], func=AF.Exp, scale=-1.0), writes=[r_gs])
    P.op("act", lambda e: e.activation(out=nlf, in_=e1, func=AF.Ln, bias=1.0, scale=1.0), writes=[r_gs])
    P.op("pe", lambda e: e.matmul(bank(3)[:, 0:64], Um, nlf, start=True, stop=True), reads=[r_gs], writes=[rbank[3]])
    ones2 = RW.f32(128)
    P.op("pool", lambda e: e.memset(ones2, 1.0), writes=[r_gs])
    P.op("pe", lambda e: e.matmul(bank(3)[:, 64:128], ones2, nlf, start=True, stop=True), reads=[r_gs], writes=[rbank[3]])
    P.op("dve", lambda e: e.tensor_tensor(out=v4(t1), in0=g3[:, :, 0:4], in1=v4(bank(3)[:, 0:64]), op=ALU.add), writes=[rbank[3], r_gs])
    P.op("dve", lambda e: e.tensor_tensor(out=t2, in0=t1, in1=bank(3)[:, 64:128], op=ALU.subtract), writes=[rbank[3], r_gs])
    LN16 = math.log(16.0)
    P.op("act", lambda e: e.activation(out=wk, in_=t1, func=AF.Exp), writes=[r_gs])
    P.op("act", lambda e: e.activation(out=ws2, in_=t2, func=AF.Exp), writes=[r_gs])
    P.op("act", lambda e: e.activation(out=einv, in_=bank(3)[:, 0:64], func=AF.Exp), writes=[rbank[3], r_gs])
    P.op("act", lambda e: e.activation(out=decay, in_=bank(3)[:, 64:128], func=AF.Exp, scale=-1.0), writes=[rbank[3], r_gs])
    P.op("dve", lambda e: e.tensor_scalar(out=wk, in0=wk, scalar1=0.0625, scalar2=None, op0=ALU.mult), writes=[r_gs])
    P.op("dve", lambda e: e.tensor_scalar(out=ws2, in0=ws2, scalar1=0.0625, scalar2=None, op0=ALU.mult), writes=[r_gs])

    maskT = RW.f32(128)
    P.op("pool", lambda e: e.memset(maskT, 1.0), writes=[r_gs])
    P.op("pool", lambda e: e.affine_select(out=maskT, in_=maskT, pattern=[[1, 128]], compare_op=ALU.is_ge, fill=0.0, base=0, channel_multiplier=-1), writes=[r_gs])
    mnw = RW.f32(1024)
    r_rows = Res()
    P.dma("sp", mnw, mnw_d.partition_broadcast(128), writes=[r_rows])
    mark2 = RW.p
    RHD = Bump(A, HD0, W0)
    qk_s = [[RW.bf16(S) for _ in range(4)], [RHD.bf16(S) for _ in range(4)]]
    r_qk_s = [[Res() for _ in range(4)] for _ in range(2)]
    va_s = [RW.bf16(16 * 258).rearrange("p (t c) -> p t c", t=16), RHD.bf16(16 * 258).rearrange("p (t c) -> p t c", t=16)]
    r_va_s = [Res(), Res()]
    pre1 = RW.f32(S + 4)
    r_pre1 = Res()
    acc = RW.f32(S)
    r_acc = Res()
    C32 = RW.f32(2 * 257).rearrange("p (a b) -> p a b", a=2)
    Cb = RW.bf16(2 * 258).rearrange("p (a b) -> p a b", a=2)
    r_C32, r_Cb = Res(), Res()
    sqk_sb = [RW.bf16(128) for _ in range(2)]
    kw_sb = [RW.bf16(256) for _ in range(2)]
    r_sqk = [Res(), Res()]
    r_kw = [Res(), Res()]
    numS = [RW.f32(258) for _ in range(3)]
    ogb = [RW.f32(256) for _ in range(2)]
    ogl = RW.f32(256)
    xx = [RW.f32(256) for _ in range(2)]
    hnn = RW.f32(256)
    hmb = [RW.bf16(256) for _ in range(2)]
    smc = [RW.f32(32) for _ in range(2)]
    r_numS = [Res(), Res(), Res()]
    r_ogb = [Res(), Res()]
    r_ogl = Res()
    r_xx = [Res(), Res()]
    r_hnn = Res()
    r_hmb = [Res(), Res()]
    r_smc = [Res(), Res()]
    bo_bf = RW.bf16(1024)
    bv_bf = RW.bf16(1024)
    ones_row = RW.bf16(128)
    mhalf2 = RW.f32(2)
    r_bo = Res()
    P.op("pool", lambda e: e.memset(mhalf2, -0.5), writes=[r_bo])
    P.dma("pool", bo_bf[0:1, :], b_in[0:1, 3072:4096], writes=[r_bo])
    P.dma("pool", bv_bf[0:1, :], b_in[0:1, 2048:3072], writes=[r_bo])
    P.op("pool", lambda e: e.memset(ones_row[0:1, :], 1.0), writes=[r_bo])
    P.op("pool", lambda e: e.memset(pre1[:, 0:4], 0.0), writes=[r_pre1])
    for si in range(2):
        P.op("pool", lambda e: e.memset(va_s[si][:, :, 256:258], 1.0), writes=[r_va_s[si]])

    def mlstm_wload(h):
        a_ = wload([w_in[:, h * 256:(h + 1) * 256], w_in[:, 1024 + h * 256:1024 + (h + 1) * 256]])
        b_ = wload([w_in[:, 2048 + h * 256:2048 + (h + 1) * 256], w_in[:, 3072 + h * 256:3072 + (h + 1) * 256]])
        return a_, b_

    def proj_ops(h, si, wA, r_wA, wB, r_wB):
        L = []

        def qk_mm(c, tc):
            bq = 0 if tc % 2 == 0 else 2

            def f():
                for kc in range(8):
                    P.op("pe", lambda e: e.matmul(bank(bq), wA[:, kc, c * 128:(c + 1) * 128], xT[:, kc, tc * 512:(tc + 1) * 512], start=(kc == 0), stop=(kc == 7)), reads=[r_wA, r_xT], writes=[rbank[bq]])
            return f

        def qk_ev(c, tc, cc):
            bq = 0 if tc % 2 == 0 else 2
            return lambda: P.op("act", lambda e: e.activation(out=pre1[:, 3 + tc * 512:3 + (tc + 1) * 512], in_=bank(bq), func=AF.Identity, bias=cvA[:, cc:cc + 1], scale=1.0), reads=[r_cv], writes=[rbank[bq], r_pre1])

        def conv_ops(c, cc):
            cwc = lambda k: cvA[:, 48 + 16 * k + cc:48 + 16 * k + cc + 1]
            return [
                lambda: P.op("act", lambda e: e.activation(out=acc, in_=pre1[:, 3:3 + S], func=AF.Identity, scale=cwc(3), bias=cvA[:, 112 + cc:113 + cc]), reads=[r_pre1, r_cv], writes=[r_acc]),
                lambda: P.op("dve", lambda e: e.scalar_tensor_tensor(out=acc[:, 0:1024], in0=pre1[:, 2:2 + 1024], scalar=cwc(2), in1=acc[:, 0:1024], op0=ALU.mult, op1=ALU.add), reads=[r_pre1, r_cv], writes=[r_acc]),
                lambda: P.op("dve", lambda e: e.scalar_tensor_tensor(out=acc[:, 1024:S], in0=pre1[:, 2 + 1024:2 + S], scalar=cwc(2), in1=acc[:, 1024:S], op0=ALU.mult, op1=ALU.add), reads=[r_pre1, r_cv], writes=[r_acc]),
                lambda: P.op("dve", lambda e: e.scalar_tensor_tensor(out=acc[:, 0:1024], in0=pre1[:, 1:1 + 1024], scalar=cwc(1), in1=acc[:, 0:1024], op0=ALU.mult, op1=ALU.add), reads=[r_pre1, r_cv], writes=[r_acc]),
                lambda: P.op("dve", lambda e: e.scalar_tensor_tensor(out=acc[:, 1024:S], in0=pre1[:, 1 + 1024:1 + S], scalar=cwc(1), in1=acc[:, 1024:S], op0=ALU.mult, op1=ALU.add), reads=[r_pre1, r_cv], writes=[r_acc]),
                lambda: P.op("dve", lambda e: e.scalar_tensor_tensor(out=acc[:, 0:1024], in0=pre1[:, 0:1024], scalar=cwc(0), in1=acc[:, 0:1024], op0=ALU.mult, op1=ALU.add), reads=[r_pre1, r_cv], writes=[r_acc]),
                lambda: P.op("dve", lambda e: e.scalar_tensor_tensor(out=acc[:, 1024:S], in0=pre1[:, 1024:S], scalar=cwc(0), in1=acc[:, 1024:S], op0=ALU.mult, op1=ALU.add), reads=[r_pre1, r_cv], writes=[r_acc]),
                lambda: P.op("act", lambda e: e.activation(out=qk_s[si][c], in_=acc, func=AF.Silu), reads=[r_acc], writes=[r_qk_s[si][c]]),
            ]

        def v_mm(t):
            bv = 2 if t % 2 == 0 else 0

            def f():
                for kc in range(8):
                    P.op("pe", lambda e: e.matmul(bank(bv)[:, 0:256], xT[:, kc, t * 128:(t + 1) * 128], wB[:, kc, 0:256], start=(kc == 0), stop=False), reads=[r_wB, r_xT], writes=[rbank[bv]])
                P.op("pe", lambda e: e.matmul(bank(bv)[:, 0:256], ones_row[0:1, :], bv_bf[0:1, h * 256:(h + 1) * 256], start=False, stop=True), reads=[r_bo], writes=[rbank[bv]])
            return f

        def v_ev(t):
            bv = 2 if t % 2 == 0 else 0
            return lambda: P.op("dve", lambda e: e.tensor_copy(out=va_s[si][:, t, 0:256], in_=bank(bv)[:, 0:256]), writes=[rbank[bv], r_va_s[si]])

        for c in range(4):
            cc = (h * 2 + c) if c < 2 else (8 + h * 2 + (c - 2))
            SPC = lambda: None
            L.append(qk_mm(c, 0))
            for tc in range(4):
                if tc + 1 < 4:
                    L.append(qk_mm(c, tc + 1))
                else:
                    L.append(SPC)
                L.append(qk_ev(c, tc, cc))
            co = conv_ops(c, cc)
            L.extend([SPC, SPC, co[0], SPC, SPC])
            L.extend(co[1:-1])
            L.extend([SPC] * 5)
            L.append(co[-1])
            L.append(v_mm(4 * c))
            for t in range(4 * c, 4 * c + 4):
                if t + 1 < 4 * c + 4:
                    L.append(v_mm(t + 1))
                L.append(v_ev(t))
        return L

    def chunk_loop(h, si, wB, r_wB, bg):
        q0, q1, k0, k1 = qk_s[si]
        rq0, rq1, rk0, rk1 = r_qk_s[si]
        vaug = va_s[si]
        r_va = r_va_s[si]

        def drip(n=1):
            for _ in range(n):
                if bg:
                    bg.pop(0)()

        def stage1(t):
            ts = slice(t * 128, (t + 1) * 128)
            col = t * 4 + h
            i = t % 2
            P.op("pe", lambda e: e.matmul(bank(3)[:, 0:128], k0[:, ts], q0[:, ts], start=True, stop=False), reads=[rk0, rq0], writes=[rbank[3]])
            P.op("pe", lambda e: e.matmul(bank(3)[:, 0:128], k1[:, ts], q1[:, ts], start=False, stop=True), reads=[rk1, rq1], writes=[rbank[3]])
            P.op("dve", lambda e: e.scalar_tensor_tensor(out=sqk_sb[i], in0=bank(3)[:, 0:128], scalar=wk[:, col:col + 1], in1=maskT, op0=ALU.mult, op1=ALU.mult), reads=[r_gs], writes=[rbank[3], r_sqk[i]])
            kps = bank_bf(4)
            P.op("pe", lambda e: e.transpose(kps[:, 0:128], k0[:, ts], ident), reads=[rk0, r_const], writes=[rbank[4]])
            P.op("pe", lambda e: e.transpose(kps[:, 128:256], k1[:, ts], ident), reads=[rk1, r_const], writes=[rbank[4]])
            P.op("act", lambda e: e.activation(out=kw_sb[i], in_=kps[:, 0:256], func=AF.Copy, scale=ws2[:, col:col + 1]), reads=[r_gs], writes=[rbank[4], r_kw[i]])
            if t >= 1:
                stage2a_pe(t - 1)
            for dc in range(2):
                P.op("pe", lambda e: e.matmul(PB[3][:, dc * 512:dc * 512 + 257], kw_sb[i][:, dc * 128:(dc + 1) * 128], vaug[:, t, 0:257], start=True, stop=True), reads=[r_kw[i], r_va], writes=[rbank[6 + dc]])
            if t > 0:
                P.op("pe", lambda e: e.matmul(bank(5)[:, 0:257], q0[:, ts], Cb[:, 0, 0:257], start=True, stop=False), reads=[rq0, r_Cb], writes=[rbank[5]])
                P.op("pe", lambda e: e.matmul(bank(5)[:, 0:257], q1[:, ts], Cb[:, 1, 0:257], start=False, stop=False), reads=[rq1, r_Cb], writes=[rbank[5]])
            P.op("pe", lambda e: e.matmul(bank(5)[:, 0:257], sqk_sb[i], vaug[:, t, 0:257], start=(t == 0), stop=True), reads=[r_sqk[i], r_va], writes=[rbank[5]])

        def stage1b(t):
            col = t * 4 + h
            dC = PB[3][:, :].rearrange("p (a b) -> p a b", a=2)[:, :, 0:257]
            if t == 0:
                P.op("dve", lambda e: e.tensor_copy(out=C32, in_=dC), writes=[rbank[6], rbank[7], r_C32])
            else:
                P.op("dve", lambda e: e.scalar_tensor_tensor(out=C32, in0=C32, scalar=decay[:, col:col + 1], in1=dC, op0=ALU.mult, op1=ALU.add), reads=[r_gs], writes=[rbank[6], rbank[7], r_C32])
            if t < NT - 1:
                P.op("act", lambda e: e.copy(out=Cb[:, :, 0:257], in_=C32), reads=[r_C32], writes=[r_Cb])
            P.op("act", lambda e: e.copy(out=numS[t % 3][:, 0:257], in_=bank(5)[:, 0:257]), writes=[rbank[5], r_numS[t % 3]])

        def stage2a_pe(t):
            ts = slice(t * 128, (t + 1) * 128)
            for kc in range(8):
                P.op("pe", lambda e: e.matmul(bank(1)[:, 0:256], xT[:, kc, ts], wB[:, kc, 256:512], start=(kc == 0), stop=False), reads=[r_wB, r_xT], writes=[rbank[1]])
            P.op("pe", lambda e: e.matmul(bank(1)[:, 0:256], ones_row[0:1, :], bo_bf[0:1, h * 256:(h + 1) * 256], start=False, stop=True), reads=[r_bo], writes=[rbank[1]])

        def stage2a(t):
            i = t % 2
            if t == NT - 1:
                stage2a_pe(t)
            P.op("act", lambda e: e.activation(out=ogb[i], in_=bank(1)[:, 0:256], func=AF.Tanh, scale=0.5), writes=[rbank[1], r_ogb[i]])

        def stage2b(t):
            col = t * 4 + h
            i = t % 2
            sm_ = smc[i]
            rs_ = r_smc[i]
            ns_ = numS[t % 3]
            rns_ = r_numS[t % 3]
            den = ns_[:, 256:257]
            P.op("dve", lambda e: e.scalar_tensor_tensor(out=sm_[:, 0:1], in0=den, scalar=-1.0, in1=den, op0=ALU.mult, op1=ALU.max), reads=[rns_], writes=[rs_])
            P.op("dve", lambda e: e.tensor_scalar(out=sm_[:, 1:2], in0=sm_[:, 0:1], scalar1=einv[:, col:col + 1], scalar2=None, op0=ALU.max), reads=[r_gs], writes=[rs_])
            P.op("dve", lambda e: e.scalar_tensor_tensor(out=sm_[:, 2:3], in0=sm_[:, 1:2], scalar=4.0 * LN_EPS, in1=sm_[:, 1:2], op0=ALU.mult, op1=ALU.mult), writes=[rs_])
            P.op("dve", lambda e: e.scalar_tensor_tensor(out=xx[i], in0=ogb[i], scalar=1.0, in1=ns_[:, 0:256], op0=ALU.add, op1=ALU.mult), reads=[rns_, r_ogb[i]], writes=[r_xx[i]])
            P.op("dve", lambda e: e.bn_stats(out=sm_[:, 8:14], in_=xx[i]), reads=[r_xx[i]], writes=[rs_])
            P.op("dve", lambda e: e.bn_aggr(out=sm_[:, 14:16], in_=sm_[:, 8:14]), writes=[rs_])
            P.op("dve", lambda e: e.tensor_tensor(out=sm_[:, 16:17], in0=sm_[:, 15:16], in1=sm_[:, 2:3], op=ALU.add), writes=[rs_])

        def stage2b_act(t):
            i = t % 2
            sm_ = smc[i]
            rs_ = r_smc[i]
            P.op("pool", lambda e: e.tensor_tensor(out=sm_[:, 17:18], in0=sm_[:, 16:17], in1=mhalf2[:, 0:1], op=ALU.pow), reads=[r_bo], writes=[rs_])

        def stage2c(t):
            ts = slice(t * 128, (t + 1) * 128)
            i = t % 2
            sm_ = smc[i]
            rs_ = r_smc[i]
            P.op("dve", lambda e: e.tensor_scalar(out=hnn, in0=xx[i], scalar1=sm_[:, 14:15], scalar2=sm_[:, 17:18], op0=ALU.subtract, op1=ALU.mult), reads=[r_xx[i], rs_], writes=[r_hnn])
            P.op("dve", lambda e: e.tensor_tensor(out=hmb[i], in0=hnn, in1=mnw[:, h * 256:(h + 1) * 256], op=ALU.mult), reads=[r_hnn, r_rows], writes=[r_hmb[i]])

        def stage2c_pe(t):
            ts = slice(t * 128, (t + 1) * 128)
            i = t % 2
            tps = bank_bf(1)
            for dc in range(2):
                P.op("pe", lambda e: e.transpose(tps[:, dc * 128:(dc + 1) * 128], hmb[i][:, dc * 128:(dc + 1) * 128], ident), reads=[r_hmb[i], r_const], writes=[rbank[1]])
            P.op("act", lambda e: e.copy(out=hmT[:, 2 * h:2 * h + 2, ts], in_=tps[:, 0:256].rearrange("p (a b) -> p a b", a=2)), writes=[rbank[1], r_hmT])

        for t in range(NT + 5):
            if 0 <= t - 5 < NT:
                stage2c_pe(t - 5)
            if t < NT:
                stage1(t)
                drip()
            if 0 <= t - 1 < NT:
                stage2a(t - 1)
                drip()
            if t < NT:
                stage1b(t)
                drip()
            if 0 <= t - 2 < NT:
                stage2b(t - 2)
                drip()
            if 0 <= t - 3 < NT:
                stage2c(t - 3)
                drip()
            if 0 <= t - 2 < NT:
                stage2b_act(t - 2)
                drip()
        while bg:
            bg.pop(0)()

    wl = mlstm_wload(0)
    for f_ in proj_ops(0, 0, wl[0][0], wl[0][1], wl[1][0], wl[1][1]):
        f_()
    for h in range(4):
        si = h % 2
        cur_wB, cur_rwB = wl[1]
        bg = []
        if h + 1 < 4:
            wl = mlstm_wload(h + 1)
            bg = proj_ops(h + 1, 1 - si, wl[0][0], wl[0][1], wl[1][0], wl[1][1])
        chunk_loop(h, si, cur_wB, cur_rwB, bg)

    P.barrier()
    RW.p = mark2

    bvd = RW.f32(1024)
    dnwc = RW.f32(2)
    mhalf = RW.f32(2)
    ones_bf = RW.bf16(128)
    ones_ff = RW.f32(128)
    P.dma("sp", bvd, b_in[0:1, 6152:7176].partition_broadcast(128), writes=[r_rows])
    P.dma("sp", dnwc[:, 0:1], dnw_d[0:1, :].rearrange("o (p k) -> (o p) k", k=1), writes=[r_rows])
    P.op("dve", lambda e: e.tensor_scalar(out=dnwc[:, 1:2], in0=dnwc[:, 0:1], scalar1=(1.0 - LAM_INIT), scalar2=None, op0=ALU.mult), writes=[r_rows])
    P.op("pool", lambda e: e.memset(mhalf, -0.5), writes=[r_rows])
    P.op("pool", lambda e: e.memset(ones_ff, 1.0), writes=[r_rows])
    P.op("dve", lambda e: e.tensor_copy(out=ones_bf, in_=ones_ff), writes=[r_rows])
    aq = RW.bf16(S)
    ak = [RW.bf16(S), RW.bf16(S)]
    r_aq, r_ak = Res(), Res()
    av = RW.bf16(16 * 256).rearrange("p (t c) -> p t c", t=16)
    r_av = Res()
    HW = 2432
    Hb1 = RW.f32(HW)
    Hb = [Hb1, Hb1]
    r_H1 = Res()
    r_H = [r_H1, r_H1]
    NE = 4
    Eb = [RW.bf16(512) for _ in range(NE)]
    r_E = [Res() for _ in range(NE)]
    E0b = [RW.bf16(512) for _ in range(NE)]
    r_E0 = [Res() for _ in range(NE)]
    EBT = [RW.bf16(HW) for _ in range(2)]
    r_EBT = [Res(), Res()]
    fin_1 = [RW.f32(512) for _ in range(4)]
    fin_t = [fin_1, fin_1]
    r_fin1 = [Res() for _ in range(4)]
    r_fin = [r_fin1, r_fin1]
    sq_t = RW.f32(512)
    rs_t = RW.f32(512)
    r_sq, r_rs = Res(), Res()
    P.op("pool", lambda e: e.memset(ak[0][64:128, :], 0.0), writes=[r_ak])
    P.op("pool", lambda e: e.memset(ak[1][0:64, :], 0.0), writes=[r_ak])

    def attn_wload(h):
        segs = [w_in[:, 4104 + h * 128:4104 + (h + 1) * 128], w_in[:, 5128 + h * 128:5128 + (h + 1) * 128]]
        if h % 2 == 0:
            segs.append(w_in[:, 6152 + h * 128:6152 + (h + 2) * 128])
        return wload(segs)

    def attn_hload(h):
        P.dma("sp", Hb[h % 2], bass.AP(gext_t, h * GL, [[1, 128], [1, HW]]), reads=[r_gext], writes=[r_H[h % 2]])

    def build_ebt(h):
        Hh_ = Hb[h % 2]
        rH_ = r_H[h % 2]
        P.op("act", lambda e: e.activation(out=Hh_, in_=Hh_, func=AF.Exp), writes=[rH_])
        for j in range(5):
            c0_, c1_ = j * 512, min(HW, (j + 1) * 512)
            bi_ = j % 2
            P.op("pe", lambda e: e.matmul(bank(bi_)[:, 0:c1_ - c0_], Jf, Hh_[:, c0_:c1_], start=True, stop=True), reads=[rH_, r_const], writes=[rbank[bi_]])
            if j % 2 == 0:
                P.op("dve", lambda e: e.tensor_copy(out=EBT[h % 2][:, c0_:c1_], in_=bank(bi_)[:, 0:c1_ - c0_]), writes=[rbank[bi_], r_EBT[h % 2]])
            else:
                P.op("act", lambda e: e.copy(out=EBT[h % 2][:, c0_:c1_], in_=bank(bi_)[:, 0:c1_ - c0_]), writes=[rbank[bi_], r_EBT[h % 2]])

    nextw = attn_wload(0)
    attn_hload(0)
    fctr = 0
    for h in range(8):
        wD, r_wD = nextw
        build_ebt(h)
        EBh = EBT[h % 2]
        rEB = r_EBT[h % 2]
        for tc in range(4):
            tsl = slice(tc * 512, (tc + 1) * 512)
            for kc in range(8):
                P.op("pe", lambda e: e.matmul(bank(0), wD[:, kc, 0:128], xT[:, kc, tsl], start=(kc == 0), stop=(kc == 7)), reads=[r_wD, r_xT], writes=[rbank[0]])
            P.op("act", lambda e: e.activation(out=aq[:, tsl], in_=bank(0), func=AF.Identity, bias=cvq[:, h:h + 1], scale=0.125), reads=[r_cv], writes=[rbank[0], r_aq])
            for kc in range(8):
                P.op("pe", lambda e: e.matmul(bank(1), wD[:, kc, 128:256], xT[:, kc, tsl], start=(kc == 0), stop=(kc == 7)), reads=[r_wD, r_xT], writes=[rbank[1]])
            P.op("dve", lambda e: e.tensor_scalar(out=ak[0][0:64, tsl], in0=bank(1)[0:64, :], scalar1=cvA[0:64, 24 + h:25 + h], scalar2=None, op0=ALU.add), reads=[r_cv], writes=[rbank[1], r_ak])
            P.op("dve", lambda e: e.tensor_scalar(out=ak[1][64:128, tsl], in0=bank(1)[64:128, :], scalar1=cvA[64:128, 24 + h:25 + h], scalar2=None, op0=ALU.add), reads=[r_cv], writes=[rbank[1], r_ak])
        if h % 2 == 0:
            for t in range(NT):
                bi = 2 + (t % 2)
                for kc in range(8):
                    P.op("pe", lambda e: e.matmul(bank(bi)[:, 0:256], xT[:, kc, t * 128:(t + 1) * 128], wD[:, kc, 256:512], start=(kc == 0), stop=(kc == 7)), reads=[r_wD, r_xT], writes=[rbank[bi]])
                P.op("dve", lambda e: e.tensor_tensor(out=av[:, t, :], in0=bank(bi)[:, 0:256], in1=bvd[:, h * 128:(h + 2) * 128], op=ALU.add), reads=[r_rows], writes=[rbank[bi], r_av])
        vo = (h % 2) * 128
        if h + 1 < 8:
            nextw = attn_wload(h + 1)
            attn_hload(h + 1)
        steps = [(qc, kt, st) for qc in range(4) for kt in range(4 * qc + 4) for st in range(2)]
        deferred = []
        since_fin = 0
        free_bank = [0]
        n = len(steps)
        LOOK = 3

        def QK(i):
            qc, kt, st = steps[i]
            bi = i % 4
            c0 = max(0, kt * 128 - qc * 512)
            P.op("pe", lambda e: e.matmul(bank(bi)[:, c0:512], ak[st][:, kt * 128:(kt + 1) * 128], aq[:, qc * 512 + c0:(qc + 1) * 512], start=True, stop=True), reads=[r_ak, r_aq], writes=[rbank[bi]])

        for i in range(-LOOK, n):
            if i + LOOK < n:
                QK(i + LOOK)
            if i < 0:
                continue
            qc, kt, st = steps[i]
            bi = i % 4
            ei = i % NE
            last = 4 * qc + 3
            free_bank[0] = bi
            c0 = max(0, kt * 128 - qc * 512)
            m0 = qc * 512 - kt * 128 + 384
            P.op("act", lambda e: e.activation(out=E0b[ei][:, c0:512], in_=bank(bi)[:, c0:512], func=AF.Exp), writes=[rbank[bi], r_E0[ei]])
            P.op("dve", lambda e: e.tensor_tensor(out=Eb[ei][:, c0:512], in0=E0b[ei][:, c0:512], in1=EBh[:, m0 + c0:m0 + 512], op=ALU.mult), reads=[r_E0[ei], rEB], writes=[r_E[ei]])
            P.op("pe", lambda e: e.matmul(bank(4 + st)[:, c0:512], av[:, kt, vo:vo + 128], Eb[ei][:, c0:512], start=(kt == 0), stop=(kt == last)), reads=[r_av, r_E[ei]], writes=[rbank[4 + st]])
            P.op("pe", lambda e: e.matmul(bank(6 + st)[:, c0:512], ones_bf, Eb[ei][:, c0:512], start=(kt == 0), stop=(kt == last)), reads=[r_rows, r_E[ei]], writes=[rbank[6 + st]])
            if kt == last and st == 1:
                o1s, o2s, z1s, z2s = fin_t[0]
                ro1, ro2, rz1, rz2 = r_fin[0]
                assert not deferred
                P.op("act", lambda e: e.copy(out=o1s, in_=bank(4)), writes=[rbank[4], ro1])
                P.op("dve", lambda e: e.tensor_copy(out=z1s, in_=bank(6)), writes=[rbank[6], rz1])
                P.op("act", lambda e: e.copy(out=o2s, in_=bank(5)), writes=[rbank[5], ro2])
                P.op("dve", lambda e: e.tensor_copy(out=z2s, in_=bank(7)), writes=[rbank[7], rz2])
                hsl = hdT[:, h, qc * 512:(qc + 1) * 512]
                DQ = deferred.append
                DQ(lambda: P.op("act", lambda e: e.activation(out=z2s, in_=z2s, func=AF.Ln), writes=[rz2]))
                DQ(lambda: P.op("act", lambda e: e.activation(out=z2s, in_=z2s, func=AF.Exp, scale=-1.0), writes=[rz2]))
                DQ(lambda: P.op("dve", lambda e: e.scalar_tensor_tensor(out=z2s, in0=z2s, scalar=neglam, in1=z1s, op0=ALU.mult, op1=ALU.mult), reads=[rz1, r_lam], writes=[rz2]))
                DQ(lambda: P.op("dve", lambda e: e.tensor_tensor(out=o2s, in0=o2s, in1=z2s, op=ALU.mult), reads=[rz2], writes=[ro2]))
                DQ(lambda: P.op("dve", lambda e: e.tensor_tensor(out=o1s, in0=o1s, in1=o2s, op=ALU.add), reads=[ro2], writes=[ro1]))
                DQ(lambda: P.op("pool", lambda e: e.tensor_tensor(out=sq_t, in0=o1s, in1=o1s, op=ALU.mult), reads=[ro1], writes=[r_sq]))
                DQ(lambda: P.op("dve", lambda e: e.scalar_tensor_tensor(out=z1s, in0=z1s, scalar=LN_EPS, in1=z1s, op0=ALU.mult, op1=ALU.mult), writes=[rz1]))
                def ss_ops():
                    fb = free_bank[0]
                    P.op("pe", lambda e: e.matmul(bank(fb), ones_ff, sq_t, start=True, stop=True), reads=[r_sq, r_rows], writes=[rbank[fb]])
                    P.op("dve", lambda e: e.scalar_tensor_tensor(out=rs_t, in0=bank(fb), scalar=1.0 / 128.0, in1=z1s, op0=ALU.mult, op1=ALU.add), reads=[rz1], writes=[rbank[fb], r_rs])
                DQ(ss_ops)
                DQ(lambda: P.op("act", lambda e: e.activation(out=rs_t, in_=rs_t, func=AF.Ln), writes=[r_rs]))
                DQ(lambda: P.op("act", lambda e: e.activation(out=rs_t, in_=rs_t, func=AF.Exp, scale=-0.5), writes=[r_rs]))
                DQ(lambda: P.op("dve", lambda e: e.scalar_tensor_tensor(out=hsl, in0=o1s, scalar=dnwc[:, 1:2], in1=rs_t, op0=ALU.mult, op1=ALU.mult), reads=[ro1, r_rs, r_rows], writes=[r_hdT]))
                since_fin = 0
            else:
                since_fin += 1
                if since_fin > 2:
                    if deferred:
                        deferred.pop(0)()
        while deferred:
            deferred.pop(0)()

    if debug:
        P.dma("pool", dbg_hm, hmT.rearrange("p c t -> p (c t)"), reads=[r_hmT])
        P.dma("pool", dbg_hd, hdT.rearrange("p c t -> p (c t)"), reads=[r_hdT])
    P.barrier()
    RW.reset()

    mergedT = RW.bf16(8 * S).rearrange("p (c t) -> p c t", c=8)
    r_mg = Res()
    mark4 = RW.p
    wslots[:] = [RW.bf16(WSLOT) for _ in range(3)]
    gmt = [RW.f32(512) for _ in range(2)]
    gdt = [RW.f32(512) for _ in range(2)]
    m1t = [RW.f32(512) for _ in range(2)]
    m2t = [RW.f32(512) for _ in range(2)]
    r_gm = [Res(), Res()]
    r_gd = [Res(), Res()]
    r_m1 = [Res(), Res()]
    r_m2 = [Res(), Res()]
    rr = 0
    def merge_wload(c):
        cs = slice(c * 128, (c + 1) * 128)
        return wload([w_bm[:, cs], w_bd[:, cs], w_in[:, 7176 + c * 128:7176 + (c + 1) * 128], w_in[:, 8200 + c * 128:8200 + (c + 1) * 128]])

    next_m = merge_wload(0)
    for c in range(8):
        wM, r_wM = next_m
        if c + 1 < 8:
            next_m = merge_wload(c + 1)
        for tc in range(4):
            b0 = (rr % 2) * 4
            i2 = rr % 2
            rr += 1
            tsl = slice(tc * 512, (tc + 1) * 512)
            srcs = [(hmT, r_hmT), (hdT, r_hdT), (xT, r_xT), (xT, r_xT)]
            for g in range(4):
                src, rs = srcs[g]
                for kc in range(8):
                    P.op("pe", lambda e, g=g, kc=kc, src=src, b0=b0, tsl=tsl: e.matmul(bank(b0 + g), wM[:, kc, g * 128:(g + 1) * 128], src[:, kc, tsl], start=(kc == 0), stop=(kc == 7)), reads=[r_wM, rs], writes=[rbank[b0 + g]])
            P.op("act", lambda e, b0=b0, i2=i2, c=c: e.activation(out=gmt[i2], in_=bank(b0 + 2), func=AF.Sigmoid, bias=cvA[:, 32 + c:33 + c], scale=1.0), reads=[r_cv], writes=[rbank[b0 + 2], r_gm[i2]])
            P.op("act", lambda e, b0=b0, i2=i2, c=c: e.activation(out=gdt[i2], in_=bank(b0 + 3), func=AF.Sigmoid, bias=cvA[:, 40 + c:41 + c], scale=1.0), reads=[r_cv], writes=[rbank[b0 + 3], r_gd[i2]])
            P.op("dve", lambda e, b0=b0, i2=i2: e.tensor_tensor(out=m1t[i2], in0=bank(b0), in1=gmt[i2], op=ALU.mult), reads=[r_gm[i2]], writes=[rbank[b0], r_m1[i2]])
            P.op("dve", lambda e, b0=b0, i2=i2: e.tensor_tensor(out=m2t[i2], in0=bank(b0 + 1), in1=gdt[i2], op=ALU.mult), reads=[r_gd[i2]], writes=[rbank[b0 + 1], r_m2[i2]])
            P.op("pool", lambda e, i2=i2, c=c, tsl=tsl: e.tensor_tensor(out=mergedT[:, c, tsl], in0=m1t[i2], in1=m2t[i2], op=ALU.add), reads=[r_m1[i2], r_m2[i2]], writes=[r_mg])

    P.barrier()
    RH = Bump(A, HM0, W0)
    wdn = RH.bf16(NFC * 1024).rearrange("p (f c) -> p f c", f=NFC)
    r_wdn = Res()
    wup = [RH.bf16(8 * 256).rearrange("p (k c) -> p k c", k=8) for _ in range(3)]
    r_wup = [Res() for _ in range(3)]

    def load_wup(f):
        i = f % 3
        P.dma("pool", wup[i][:, :, 0:128], w_up[:, f * 128:(f + 1) * 128].rearrange("(k p) c -> p k c", p=128), writes=[r_wup[i]])
        P.dma("pool", wup[i][:, :, 128:256], w_up[:, DFF + f * 128:DFF + (f + 1) * 128].rearrange("(k p) c -> p k c", p=128), writes=[r_wup[i]])

    RW.p = mark4
    h1T = xT
    r_h1T = Res()
    wo = RW.bf16(8 * 1024).rearrange("p (k c) -> p k c", k=8)
    r_wo = Res()
    for hh in range(2):
        P.dma("pool", wo[:, :, hh * 512:(hh + 1) * 512], w_out[:, hh * 512:(hh + 1) * 512].rearrange("(k p) c -> p k c", p=128), writes=[r_wo])
    load_wup(0)
    for f in range(NFC):
        P.dma("pool", wdn[:, f, :], w_dn[f * 128:(f + 1) * 128, :], writes=[r_wdn])
    lng = RW.f32(1024)
    lnb = RW.f32(1024)
    r_ln = Res()
    P.dma("sp", lng, ln1g_d.partition_broadcast(128), writes=[r_ln])
    P.dma("sp", lnb, ln1b_d.partition_broadcast(128), writes=[r_ln])
    NB4 = 3
    xres = [RW.f32(1024) for _ in range(NB4)]
    yt = [RW.f32(1024) for _ in range(NB4)]
    hbt = [RW.bf16(1024) for _ in range(NB4)]
    lsm = [RW.f32(32) for _ in range(NB4)]
    r_xres = [Res() for _ in range(NB4)]
    r_yt = [Res() for _ in range(NB4)]
    r_hbt = [Res() for _ in range(NB4)]
    r_lsm = [Res() for _ in range(NB4)]

    def interleave(lists, skew):
        n_ = len(lists)
        pos = [0] * n_
        step = 0
        while any(pos[k] < len(lists[k]) for k in range(n_)):
            for k in range(n_):
                if step >= k * skew and pos[k] < len(lists[k]):
                    lists[k][pos[k]]()
                    pos[k] += 1
            step += 1

    def ln_ops(y_, s_, r_y, r_s, xr_, r_xr, ps_ap, ps_res, g_row, b_row, r_rowsln):
        L = []
        L.append(lambda: P.op("dve", lambda e: e.scalar_tensor_tensor(out=y_, in0=xr_, scalar=ALPHA, in1=ps_ap, op0=ALU.mult, op1=ALU.add), reads=[r_xr], writes=list(ps_res) + [r_y]))
        L.append(lambda: P.op("dve", lambda e: e.bn_stats(out=s_[:, 0:6], in_=y_[:, 0:512]), reads=[r_y], writes=[r_s]))
        L.append(lambda: P.op("dve", lambda e: e.bn_stats(out=s_[:, 6:12], in_=y_[:, 512:1024]), reads=[r_y], writes=[r_s]))
        L.append(lambda: P.op("dve", lambda e: e.bn_aggr(out=s_[:, 12:14], in_=s_[:, 0:12]), writes=[r_s]))
        L.append(lambda: P.op("act", lambda e: e.activation(out=s_[:, 14:15], in_=s_[:, 13:14], func=AF.Ln, bias=LN_EPS, scale=1.0), writes=[r_s]))
        L.append(lambda: P.op("act", lambda e: e.activation(out=s_[:, 15:16], in_=s_[:, 14:15], func=AF.Exp, scale=-0.5), writes=[r_s]))
        L.append(lambda: P.op("dve", lambda e: e.scalar_tensor_tensor(out=s_[:, 16:17], in0=s_[:, 12:13], scalar=-1.0, in1=s_[:, 15:16], op0=ALU.mult, op1=ALU.mult), writes=[r_s]))
        L.append(lambda: P.op("act", lambda e: e.activation(out=y_, in_=y_, func=AF.Identity, scale=s_[:, 15:16], bias=s_[:, 16:17]), reads=[r_s], writes=[r_y]))
        L.append(lambda: P.op("dve", lambda e: e.tensor_tensor(out=y_, in0=y_, in1=g_row, op=ALU.mult), reads=[r_rowsln], writes=[r_y]))
        L.append(lambda: P.op("dve", lambda e: e.tensor_tensor(out=y_, in0=y_, in1=b_row, op=ALU.add), reads=[r_rowsln], writes=[r_y]))
        return L

    out_dmas = []

    def tile4b(t):
        i = t % NB4
        ts = slice(t * 128, (t + 1) * 128)
        pb = PB[i]
        L = []
        L.append(lambda: P.dma("sp", xres[i], x[ts, :], writes=[r_xres[i]]))

        def mm(hh):
            for kc in range(8):
                P.op("pe", lambda e: e.matmul(pb[:, hh * 512:(hh + 1) * 512], mergedT[:, kc, ts], wo[:, kc, hh * 512:(hh + 1) * 512], start=(kc == 0), stop=(kc == 7)), reads=[r_mg, r_wo], writes=[rbank[2 * i + hh]])
        L.append(lambda: mm(0))
        L.append(lambda: mm(1))
        L.extend(ln_ops(yt[i], lsm[i], r_yt[i], r_lsm[i], xres[i], r_xres[i], pb[:, :], [rbank[2 * i], rbank[2 * i + 1]], lng, lnb, r_ln))
        L.append(lambda: P.dma("sp", h1s[ts, :], yt[i], reads=[r_yt[i]]))
        L.append(lambda: P.op("act", lambda e: e.copy(out=hbt[i], in_=yt[i]), reads=[r_yt[i]], writes=[r_hbt[i]]))
        tb = 6 + (t % 2)
        tps = bank_bf(tb)

        def trs():
            for kc in range(8):
                P.op("pe", lambda e: e.transpose(tps[:, kc * 128:(kc + 1) * 128], hbt[i][:, kc * 128:(kc + 1) * 128], ident), reads=[r_hbt[i], r_const], writes=[rbank[tb]])
        L.append(trs)
        L.append(lambda: P.op("dve", lambda e: e.tensor_copy(out=h1T[:, :, ts], in_=tps.rearrange("p (c t) -> p c t", c=8)), writes=[rbank[tb], r_h1T]))
        return L

    interleave([tile4b(t) for t in range(NT)], 6)

    P.barrier()

    RW.reset()
    gT = RW.bf16(NFC * 1024).rearrange("p (f t) -> p f t", f=NFC)
    r_gT = Res()
    lng2 = RW.f32(1024)
    lnb2 = RW.f32(1024)
    r_ln2 = Res()
    P.dma("sp", lng2, ln2g_d.partition_broadcast(128), writes=[r_ln2])
    P.dma("sp", lnb2, ln2b_d.partition_broadcast(128), writes=[r_ln2])
    xres5 = [RW.f32(1024) for _ in range(3)]
    yt5 = [RW.f32(1024) for _ in range(3)]
    lsm5 = [RW.f32(32) for _ in range(3)]
    r_xres5 = [Res() for _ in range(3)]
    r_yt5 = [Res() for _ in range(3)]
    r_lsm5 = [Res() for _ in range(3)]
    abuf = [RW.f32(1028) for _ in range(2)]
    ybuf = [RW.f32(1024) for _ in range(2)]
    sbuf_ = [RW.f32(1024) for _ in range(2)]
    r_ab = [Res(), Res()]
    r_yb = [Res(), Res()]
    r_sb = [Res(), Res()]
    halo = RW.f32(NFC * 2).rearrange("p (f k) -> p f k", f=NFC)
    r_halo = Res()
    for i in range(2):
        P.op("pool", lambda e, i=i: e.memset(abuf[i][:, 0:2], 0.0), writes=[r_ab[i]])

    for hf in range(2):
        if hf == 1:
            load_wup(0)
        for f in range(NFC):
            if f + 1 < NFC:
                load_wup(f + 1)
            wi = f % 3
            ai = f % 2
            bb0 = 2 + 2 * (f % 2)
            if hf == 1:
                P.op("dve", lambda e, ai=ai, f=f: e.tensor_copy(out=abuf[ai][:, 0:2], in_=halo[:, f, :]), reads=[r_halo], writes=[r_ab[ai]])
            for tc2 in range(2):
                tsl = slice(hf * 1024 + tc2 * 512, hf * 1024 + (tc2 + 1) * 512)
                for kc in range(8):
                    P.op("pe", lambda e, kc=kc, wi=wi, tc2=tc2, tsl=tsl: e.matmul(bank(tc2), wup[wi][:, kc, 0:128], h1T[:, kc, tsl], start=(kc == 0), stop=(kc == 7)), reads=[r_wup[wi], r_h1T], writes=[rbank[tc2]])
                P.op("act", lambda e, tc2=tc2, ai=ai: e.copy(out=abuf[ai][:, 2 + tc2 * 512:2 + (tc2 + 1) * 512], in_=bank(tc2)), writes=[rbank[tc2], r_ab[ai]])
                for kc in range(8):
                    P.op("pe", lambda e, kc=kc, wi=wi, tc2=tc2, tsl=tsl, bb0=bb0: e.matmul(bank(bb0 + tc2), wup[wi][:, kc, 128:256], h1T[:, kc, tsl], start=(kc == 0), stop=(kc == 7)), reads=[r_wup[wi], r_h1T], writes=[rbank[bb0 + tc2]])
            fw = lambda k, f=f: cvB[:, 22 * k + f:22 * k + f + 1]
            P.op("act", lambda e, ai=ai, f=f, fw=fw: e.activation(out=ybuf[ai], in_=abuf[ai][:, 2:1026], func=AF.Identity, scale=fw(2), bias=cvB[:, 66 + f:67 + f]), reads=[r_ab[ai], r_cv], writes=[r_yb[ai]])
            P.op("dve", lambda e, ai=ai, fw=fw: e.scalar_tensor_tensor(out=ybuf[ai], in0=abuf[ai][:, 1:1025], scalar=fw(1), in1=ybuf[ai], op0=ALU.mult, op1=ALU.add), reads=[r_ab[ai], r_cv], writes=[r_yb[ai]])
            P.op("dve", lambda e, ai=ai, fw=fw: e.scalar_tensor_tensor(out=ybuf[ai], in0=abuf[ai][:, 0:1024], scalar=fw(0), in1=ybuf[ai], op0=ALU.mult, op1=ALU.add), reads=[r_ab[ai], r_cv], writes=[r_yb[ai]])
            if hf == 0:
                P.op("pool", lambda e, ai=ai, f=f: e.tensor_copy(out=halo[:, f, :], in_=abuf[ai][:, 1024:1026]), reads=[r_ab[ai]], writes=[r_halo])
            P.op("act", lambda e, ai=ai: e.activation(out=sbuf_[ai], in_=ybuf[ai], func=AF.Silu), reads=[r_yb[ai]], writes=[r_sb[ai]])
            for tc2 in range(2):
                P.op("dve", lambda e, ai=ai, tc2=tc2, f=f, bb0=bb0: e.tensor_tensor(out=gT[:, f, tc2 * 512:(tc2 + 1) * 512], in0=bank(bb0 + tc2), in1=sbuf_[ai][:, tc2 * 512:(tc2 + 1) * 512], op=ALU.mult), reads=[r_sb[ai]], writes=[rbank[bb0 + tc2], r_gT])
        def tile5(tl, hf=hf):
            t = hf * 8 + tl
            i = t % 3
            ts = slice(t * 128, (t + 1) * 128)
            tls = slice(tl * 128, (tl + 1) * 128)
            pb = PB[i]
            pres = [rbank[2 * i], rbank[2 * i + 1]]
            L = []
            L.append(lambda: P.dma("sp", xres5[i], h1s[ts, :], writes=[r_xres5[i]]))

            def mm(hh):
                for f in range(NFC):
                    P.op("pe", lambda e: e.matmul(pb[:, hh * 512:(hh + 1) * 512], gT[:, f, tls], wdn[:, f, hh * 512:(hh + 1) * 512], start=(f == 0), stop=(f == NFC - 1)), reads=[r_gT, r_wdn], writes=[pres[hh]])
            L.append(lambda: mm(0))
            L.append(lambda: mm(1))
            L.extend(ln_ops(yt5[i], lsm5[i], r_yt5[i], r_lsm5[i], xres5[i], r_xres5[i], pb[:, :], pres, lng2, lnb2, r_ln2))
            L.append(lambda: out_dmas.append(P.dma("sp", out[ts, :], yt5[i], reads=[r_yt5[i]])))
            return L

        interleave([tile5(tl) for tl in range(8)], 5)

    P.op("sp", None, extra=out_dmas + list(P.dmas_since_barrier))
    P.finalize_and_emit()
    return nc


_NC_CACHE = {}


def _get_nc(debug=False):
    if debug not in _NC_CACHE:
        _NC_CACHE[debug] = build_program(debug)
    return _NC_CACHE[debug]


def kernel(x, w_in, b_in, mlstm_conv_w, mlstm_conv_b, mlstm_norm_w, lambda_q1, lambda_k1, lambda_q2, lambda_k2,
           diff_norm_w, rel_bias, w_branch_mlstm, w_branch_diff, w_out, ln1_g, ln1_b, w_ffn_up, ffn_conv_w,
           ffn_conv_b, w_ffn_down, ln2_g, ln2_b, _debug=False):
    f = lambda a: np.ascontiguousarray(np.asarray(a, dtype=np.float32))
    x = f(x)
    shared = {
        "w_in": f(w_in)[0], "b_in": f(b_in), "mlstm_conv_w": f(mlstm_conv_w)[0], "mlstm_conv_b": f(mlstm_conv_b),
        "mlstm_norm_w": f(mlstm_norm_w), "lambda_q1": f(lambda_q1), "lambda_k1": f(lambda_k1),
        "lambda_q2": f(lambda_q2), "lambda_k2": f(lambda_k2), "diff_norm_w": f(diff_norm_w),
        "rel_bias": f(rel_bias), "w_branch_mlstm": f(w_branch_mlstm)[0], "w_branch_diff": f(w_branch_diff)[0],
        "w_out": f(w_out)[0], "ln1_g": f(ln1_g), "ln1_b": f(ln1_b), "w_ffn_up": f(w_ffn_up)[0],
        "ffn_conv_w": f(ffn_conv_w)[0], "ffn_conv_b": f(ffn_conv_b), "w_ffn_down": f(w_ffn_down)[0],
        "ln2_g": f(ln2_g), "ln2_b": f(ln2_b), "oh": onehot_table(),
    }
    nc = _get_nc(_debug)
    ncores = x.shape[0]
    in_maps = [dict(shared, x=x[b]) for b in range(ncores)]
    res = run_bass_kernel_spmd(nc, in_maps, core_ids=list(range(ncores)))
    if _debug:
        return res
    return np.stack([r["out"] for r in res.results], axis=0)
```
